# Optimizing a Trainium2 kernel written in Bass

```python
import math
import jax, jax.numpy as jnp
from jax import lax
import numpy as np

D_MODEL = 1024
BATCH = 2
SEQ = 8192
DEPTH = 1

HEAD_DIM = 64
N_HEADS_A = 8
N_KV_A = 2
GQA_GROUP = N_HEADS_A // N_KV_A
N_HEADS_B = 8
WIDTH_A = N_HEADS_A * HEAD_DIM
WIDTH_B = N_HEADS_B * HEAD_DIM
KV_WIDTH_A = N_KV_A * HEAD_DIM
IN_WIDTH = WIDTH_A + 2 * KV_WIDTH_A + 3 * WIDTH_B
D_FF = 2816
CONV_WIDTH = 3
GRID_W = 64
ROPE_THETA = 10000.0
ROPE_AXIS_DIM = HEAD_DIM // 2
Q_BLOCK = 128
DIL_BLOCK = 64
DILATED_PATTERNS = ((128, 1), (512, 4), (2048, 16))
NORM_EPS = 1e-6
NEG_INF = -1e30

kernel_name = "hybrid_gqa_axialrope_dilated_alibi_convffn"


def rms_norm(x, g):
    xf = x.astype(jnp.float32)
    y = xf * lax.rsqrt(jnp.mean(xf * xf, axis=-1, keepdims=True) + NORM_EPS)
    return (y * g.astype(jnp.float32)).astype(x.dtype)


def axial_rope_tables(seq_len):
    rows = seq_len // GRID_W
    row = jnp.repeat(jnp.arange(rows, dtype=jnp.float32), GRID_W)
    col = jnp.tile(jnp.arange(GRID_W, dtype=jnp.float32), rows)
    inv = ROPE_THETA ** (-jnp.arange(0, ROPE_AXIS_DIM, 2, dtype=jnp.float32) / ROPE_AXIS_DIM)
    ang_r = row[:, None] * inv[None, :]
    ang_c = col[:, None] * inv[None, :]
    return jnp.cos(ang_r), jnp.sin(ang_r), jnp.cos(ang_c), jnp.sin(ang_c)


def _rotate(u, c, s):
    half = u.shape[-1] // 2
    u1, u2 = u[..., :half], u[..., half:]
    c = c[:, None, :]
    s = s[:, None, :]
    return jnp.concatenate([u1 * c - u2 * s, u2 * c + u1 * s], axis=-1)


def apply_axial_rope(x, tables):
    cos_r, sin_r, cos_c, sin_c = tables
    xf = x.astype(jnp.float32)
    out = jnp.concatenate([_rotate(xf[..., :ROPE_AXIS_DIM], cos_r, sin_r),
                           _rotate(xf[..., ROPE_AXIS_DIM:], cos_c, sin_c)], axis=-1)
    return out.astype(x.dtype)


def global_gqa_attention(q, k, v):
    b, s = q.shape[0], q.shape[1]
    nb = s // Q_BLOCK
    qg = q.reshape(b, nb, Q_BLOCK, N_KV_A, GQA_GROUP, HEAD_DIM).transpose(1, 0, 2, 3, 4, 5)
    scale = HEAD_DIM ** -0.5

    def block(qb):
        sc = jnp.einsum('bqkgd,bskd->bkgqs', qb, k, preferred_element_type=jnp.float32) * scale
        p = jax.nn.softmax(sc, axis=-1)
        return jnp.einsum('bkgqs,bskd->bqkgd', p.astype(v.dtype), v)

    o = lax.map(block, qg)
    return o.transpose(1, 0, 2, 3, 4, 5).reshape(b, s, WIDTH_A)


def alibi_slopes(n_heads):
    return 2.0 ** (-8.0 * jnp.arange(1, n_heads + 1, dtype=jnp.float32) / n_heads)


def dilated_window_attention(q, k, v, window, dilation, slopes):
    b, s, h, hd = q.shape
    d = dilation
    n_side = (window // 2) // d
    L = s // d
    nb = -(-L // DIL_BLOCK)
    Lp = nb * DIL_BLOCK

    def strided(a):
        return a.reshape(b, L, d, h, hd).transpose(0, 2, 1, 3, 4)

    qs = jnp.pad(strided(q), ((0, 0), (0, 0), (0, Lp - L), (0, 0), (0, 0)))
    qb = qs.reshape(b, d, nb, DIL_BLOCK, h, hd)

    def banded(a):
        ap = jnp.pad(strided(a), ((0, 0), (0, 0), (DIL_BLOCK, Lp - L + DIL_BLOCK), (0, 0), (0, 0)))
        ap = ap.reshape(b, d, nb + 2, DIL_BLOCK, h, hd)
        return jnp.concatenate([ap[:, :, :-2], ap[:, :, 1:-1], ap[:, :, 2:]], axis=3)

    kw = banded(k)
    vw = banded(v)

    qi = jnp.arange(DIL_BLOCK)[:, None]
    kj = jnp.arange(3 * DIL_BLOCK)[None, :]
    off = kj - DIL_BLOCK - qi
    key_idx = jnp.arange(nb)[:, None, None] * DIL_BLOCK - DIL_BLOCK + kj[None]
    valid = (jnp.abs(off) <= n_side)[None] & (key_idx >= 0) & (key_idx < L)
    dist = (jnp.abs(off) * d).astype(jnp.float32)
    bias = jnp.where(valid[:, None], -slopes[None, :, None, None] * dist[None, None],
                     NEG_INF)

    sc = jnp.einsum('brnqhd,brnkhd->brnhqk', qb, kw, preferred_element_type=jnp.float32)
    sc = sc * (hd ** -0.5) + bias[None, None]
    lse = jax.nn.logsumexp(sc, axis=-1)
    p = jnp.exp(sc - lse[..., None])
    o = jnp.einsum('brnhqk,brnkhd->brnqhd', p.astype(v.dtype), vw)

    o = o.reshape(b, d, Lp, h, hd)[:, :, :L].transpose(0, 2, 1, 3, 4).reshape(b, s, h, hd)
    lse = lse.transpose(0, 1, 2, 4, 3).reshape(b, d, Lp, h)[:, :, :L]
    lse = lse.transpose(0, 2, 1, 3).reshape(b, s, h)
    return o, lse


def dilated_mixture_attention(q, k, v):
    slopes = alibi_slopes(N_HEADS_B)
    outs, lses = [], []
    for window, dilation in DILATED_PATTERNS:
        o, lse = dilated_window_attention(q, k, v, window, dilation, slopes)
        outs.append(o)
        lses.append(lse)
    w = jax.nn.softmax(jnp.stack(lses, axis=0), axis=0)
    o = jnp.sum(w[..., None].astype(q.dtype) * jnp.stack(outs, axis=0), axis=0)
    return o.reshape(q.shape[0], q.shape[1], WIDTH_B)


def conv_ffn(h, w_up, conv_w, conv_b, w_down):
    u = h @ w_up
    c = u.shape[-1]
    u = lax.conv_general_dilated(u, conv_w[:, None, :].astype(u.dtype), window_strides=(1,),
                                 padding=((CONV_WIDTH // 2, CONV_WIDTH // 2),),
                                 dimension_numbers=('NWC', 'WIO', 'NWC'),
                                 feature_group_count=c) + conv_b
    gate, val = u[..., :D_FF], u[..., D_FF:]
    return (jax.nn.gelu(gate) * val) @ w_down


def setup_inputs(seed: int = 0) -> dict:
    key = jax.random.key(seed)
    ks = jax.random.split(key, 16)
    f32 = jnp.float32

    def gain(k, n):
        return 1.0 + 0.05 * jax.random.normal(k, (n,), f32)

    return {
        "x": jax.random.normal(ks[0], (BATCH, SEQ, D_MODEL), f32),
        "norm1_g": gain(ks[1], D_MODEL),
        "w_in": jax.random.normal(ks[2], (D_MODEL, IN_WIDTH), f32) * D_MODEL ** -0.5,
        "qa_norm_g": gain(ks[3], HEAD_DIM),
        "ka_norm_g": gain(ks[4], HEAD_DIM),
        "qb_norm_g": gain(ks[5], HEAD_DIM),
        "kb_norm_g": gain(ks[6], HEAD_DIM),
        "outa_norm_g": gain(ks[7], WIDTH_A),
        "outb_norm_g": gain(ks[8], WIDTH_B),
        "w_out": jax.random.normal(ks[9], (WIDTH_A + WIDTH_B, D_MODEL), f32) * (WIDTH_A + WIDTH_B) ** -0.5,
        "norm2_g": gain(ks[10], D_MODEL),
        "w_up": jax.random.normal(ks[11], (D_MODEL, 2 * D_FF), f32) * D_MODEL ** -0.5,
        "conv_w": jax.random.normal(ks[12], (CONV_WIDTH, 2 * D_FF), f32) * CONV_WIDTH ** -0.5,
        "conv_b": 0.02 * jax.random.normal(ks[13], (2 * D_FF,), f32),
        "w_down": jax.random.normal(ks[14], (D_FF, D_MODEL), f32) * D_FF ** -0.5,
    }


def reference(x, norm1_g, w_in, qa_norm_g, ka_norm_g, qb_norm_g, kb_norm_g,
              outa_norm_g, outb_norm_g, w_out, norm2_g, w_up, conv_w, conv_b, w_down):
    b, s, _ = x.shape
    tables = axial_rope_tables(s)
    for _layer in range(DEPTH):
        hn = rms_norm(x, norm1_g)
        proj = hn @ w_in
        o1 = WIDTH_A
        o2 = o1 + KV_WIDTH_A
        o3 = o2 + KV_WIDTH_A
        o4 = o3 + WIDTH_B
        o5 = o4 + WIDTH_B
        qa = proj[..., :o1].reshape(b, s, N_HEADS_A, HEAD_DIM)
        ka = proj[..., o1:o2].reshape(b, s, N_KV_A, HEAD_DIM)
        va = proj[..., o2:o3].reshape(b, s, N_KV_A, HEAD_DIM)
        qb = proj[..., o3:o4].reshape(b, s, N_HEADS_B, HEAD_DIM)
        kb = proj[..., o4:o5].reshape(b, s, N_HEADS_B, HEAD_DIM)
        vb = proj[..., o5:].reshape(b, s, N_HEADS_B, HEAD_DIM)

        qa = apply_axial_rope(rms_norm(qa, qa_norm_g), tables)
        ka = apply_axial_rope(rms_norm(ka, ka_norm_g), tables)
        out_a = global_gqa_attention(qa, ka, va)

        out_b = dilated_mixture_attention(rms_norm(qb, qb_norm_g), rms_norm(kb, kb_norm_g), vb)

        mixed = jnp.concatenate([rms_norm(out_a, outa_norm_g), rms_norm(out_b, outb_norm_g)], axis=-1)
        x = x + mixed @ w_out

        x = x + conv_ffn(rms_norm(x, norm2_g), w_up, conv_w, conv_b, w_down)
    return x
```

```python
import math
from collections import defaultdict
from contextlib import ExitStack

import numpy as np
import ml_dtypes

import concourse.bass as bass
import concourse.mybir as mybir
from concourse.bass_utils import run_bass_kernel_spmd

F32 = mybir.dt.float32
BF16 = mybir.dt.bfloat16
AF = mybir.ActivationFunctionType
ALU = mybir.AluOpType

HALO = 1152
C0 = 1151
MW = 2430
EPS = 1e-6


class Op:
    __slots__ = ("eng", "fn", "reads", "writes", "dma", "deps", "sig", "needs_sig", "prev_dma")

    def __init__(self, eng, fn, reads, writes, dma):
        self.eng, self.fn, self.reads, self.writes, self.dma = eng, fn, reads, writes, dma
        self.deps = []
        self.sig = None
        self.needs_sig = False
        self.prev_dma = None


class Sched:
    def __init__(self, n_dma_sems=8):
        self.ops = []
        self.last_writer = {}
        self.readers = defaultdict(list)
        self.n_dma_sems = n_dma_sems
        self.last_on_eng = {}
        self.pending_barrier = {}
        self.outstanding_dma = []

    def barrier(self):
        deps = list(self.last_on_eng.values()) + list(self.outstanding_dma)
        self.outstanding_dma = []
        for e in ("pe", "act", "dve", "pool", "sp"):
            self.pending_barrier[e] = list(self.pending_barrier.get(e, [])) + deps

    def add(self, eng, fn, r=(), w=(), dma=False):
        op = Op(eng, fn, tuple(r), tuple(w), dma)
        deps = set()
        for b in op.reads:
            lw = self.last_writer.get(b)
            if lw is not None:
                deps.add(lw)
        for b in op.writes:
            lw = self.last_writer.get(b)
            if lw is not None:
                deps.add(lw)
            for rd in self.readers.get(b, ()):
                deps.add(rd)
        fdeps = []
        for d in deps:
            if d.eng == op.eng and not d.dma and not op.dma:
                if op.eng == "pe":
                    continue
                if op.eng != "pool" and not any(b in d.writes for b in op.reads):
                    continue
            fdeps.append(d)
        pb = self.pending_barrier.pop(eng, None)
        if pb:
            for d in pb:
                if d is not op and (d.dma or d.eng != eng):
                    fdeps.append(d)
        op.deps = fdeps
        for d in fdeps:
            d.needs_sig = True
        for b in op.reads:
            self.readers[b].append(op)
        for b in op.writes:
            self.last_writer[b] = op
            self.readers[b] = []
        self.ops.append(op)
        if dma:
            self.outstanding_dma.append(op)
        else:
            self.last_on_eng[eng] = op
        return op

    def emit(self, nc, stack, final_wait_ops=()):
        sems = {e: stack.enter_context(nc.semaphore("s_" + e)) for e in ("pe", "act", "dve", "pool")}
        dma_sems = [stack.enter_context(nc.semaphore("d_sp%d" % i)) for i in range(self.n_dma_sems)]
        cnt = defaultdict(int)
        rr = 0
        dcnt = defaultdict(int)
        dprev = {}
        for op in self.ops:
            if op.dma:
                i = rr % self.n_dma_sems
                rr += 1
                dcnt[i] += 1
                op.sig = (dma_sems[i], 16 * dcnt[i], 16)
                op.prev_dma = dprev.get(i)
                dprev[i] = op
            elif op.needs_sig:
                cnt[op.eng] += 1
                op.sig = (sems[op.eng], cnt[op.eng], 1)
        per_eng = defaultdict(list)
        for op in self.ops:
            per_eng[op.eng].append(op)
        block = stack.enter_context(nc.Block())

        def run(e, name):
            waited = {}

            def wait(sig):
                sem, val, _ = sig
                k = id(sem)
                if waited.get(k, 0) < val:
                    e.wait_ge(sem, val)
                    waited[k] = val

            for op in per_eng[name]:
                for d in op.deps:
                    wait(d.sig)
                if op.dma and op.prev_dma is not None:
                    wait(op.prev_dma.sig)
                ins = op.fn(e)
                if op.sig is not None:
                    ins.then_inc(op.sig[0], op.sig[2])
            if name == "sp":
                for op in final_wait_ops:
                    wait(op.sig)

        @block.tensor
        def _(e):
            run(e, "pe")

        @block.scalar
        def _(e):
            run(e, "act")

        @block.vector
        def _(e):
            run(e, "dve")

        @block.gpsimd
        def _(e):
            run(e, "pool")

        @block.sync
        def _(e):
            run(e, "sp")


def build(D, SEQ, NT, DFF):
    KC = D // 128
    NB = D // 512
    WIN = NT + 2 * HALO
    NWT = WIN // 128
    E = NT + 2
    NJ = DFF // 128
    NKT = SEQ // 128
    HN = NT // 2
    CH = 256

    nc = bass.Bass("TRN2", target_bir_lowering=False)

    def din(name, shape, dt=F32):
        return nc.dram_tensor(name, list(shape), dt, kind="ExternalInput").ap()

    xfull = din("xfull", [SEQ, D])
    xwin = din("xwin", [WIN, D])
    w_inB = din("w_inB", [D, 2560])
    w_inA = din("w_inA", [D, 384])
    gq_d = din("gq", [128, 6])
    g1_d = din("g1", [D])
    ropeq_d = din("ropeq", [2, 128, E])
    ropek_d = din("ropek", [2, 128, SEQ])
    maskB_d = din("maskB", [128, 8, MW], BF16)
    vones_d = din("vones", [128, NWT, 8])
    halo_d = din("halo_m", [128, 2])
    w_out_d = din("w_out", [1024, D])
    gg_d = din("gg", [1024])
    g2_d = din("g2", [D])
    w_up_d = din("w_up", [D, 2 * DFF])
    convw_d = din("convw", [128, 2 * NJ, 3])
    convb_d = din("convb", [128, 2 * NJ])
    w_down_d = din("w_down", [DFF, D])
    y_d = nc.dram_tensor("y", [NT, D], F32, kind="ExternalOutput").ap()
    xmid_d = nc.dram_tensor("xmid_scr", [E, D], F32, kind="Internal").ap()

    S = Sched()
    A = S.add
    out_ops = []

    with ExitStack() as st:
        ARENA = 212736
        arena = st.enter_context(nc.sbuf_tensor("arena", [128, ARENA // 2], BF16))
        psum = st.enter_context(nc.psum_tensor("psum", [128, 4096], F32))
        psum_bf = psum[:].bitcast(BF16)

        class Bump:
            def __init__(self):
                self.off = 0

            def alloc(self, shape, dt):
                esz = 4 if dt == F32 else 2
                n = int(np.prod(shape[1:]))
                nb = (n * esz + 31) // 32 * 32
                assert self.off + nb <= ARENA, ("SBUF overflow", self.off + nb)
                ap = arena[:, self.off // 2: self.off // 2 + n * esz // 2]
                self.off += nb
                if dt == F32:
                    ap = ap.bitcast(F32)
                if len(shape) == 3:
                    ap = ap.rearrange("p (a b) -> p a b", a=shape[1])
                elif len(shape) == 4:
                    ap = ap.rearrange("p (a b c) -> p a b c", a=shape[1], b=shape[2])
                return ap

        M = Bump()

        def bank(b, n=512, off=0):
            return psum[:, b * 512 + off: b * 512 + off + n]

        def bank_bf(b, n):
            return psum_bf[:, b * 1024: b * 1024 + n]

        uid = [0]

        def key(prefix):
            uid[0] += 1
            return (prefix, uid[0])

        ident = M.alloc([128, 128], BF16)
        identf = M.alloc([128, 128], F32)
        bones = M.alloc([128, 128], BF16)
        bonesf = M.alloc([128, 128], F32)
        gq = M.alloc([128, 6], F32)
        g1rep = M.alloc([128, D], F32)
        halo_m = M.alloc([128, 2], F32)
        A("pool", lambda e: e.memset(identf, 0.0), w=["identf"])
        A("pool", lambda e: e.affine_select(out=identf, in_=identf, pattern=[[-1, 128]], compare_op=ALU.not_equal,
                                            fill=1.0, base=0, channel_multiplier=1), r=["identf"], w=["identf"])
        A("pool", lambda e: e.tensor_copy(out=ident, in_=identf), r=["identf"], w=["ident"])
        A("pool", lambda e: e.memset(bonesf, 0.0), w=["bonesf"])
        A("pool", lambda e: e.memset(bonesf[0:64, 0:64], 1.0), r=["bonesf"], w=["bonesf"])
        A("pool", lambda e: e.memset(bonesf[64:128, 64:128], 1.0), r=["bonesf"], w=["bonesf"])
        A("pool", lambda e: e.tensor_copy(out=bones, in_=bonesf), r=["bonesf"], w=["bones"])
        A("sp", lambda e: e.dma_start(out=gq, in_=gq_d), w=["gq"], dma=True)
        A("sp", lambda e: e.dma_start(out=g1rep, in_=g1_d.partition_broadcast(128)), w=["g1rep"], dma=True)
        A("sp", lambda e: e.dma_start(out=halo_m, in_=halo_d), w=["halo_m"], dma=True)

        stage = [M.alloc([128, 512], F32) for _ in range(2)]
        QAT = M.alloc([128, 4, E], BF16)
        mark_persist = M.off

        cast_rr = [0]

        def load_cast(dst, src, rows, ncols, dkey, stage):
            for kc in range(rows // 128):
                for c0 in range(0, ncols, 512):
                    cn = min(512, ncols - c0)
                    i = cast_rr[0] % 2
                    cast_rr[0] += 1
                    sk = ("stage", i)
                    A("sp", lambda e, kc=kc, c0=c0, cn=cn, i=i: e.dma_start(
                        out=stage[i][:, 0:cn], in_=src[kc * 128:(kc + 1) * 128, c0:c0 + cn]), w=[sk], dma=True)
                    eng = ("pool", "act")[cast_rr[0] % 2]
                    if eng == "pool":
                        A("pool", lambda e, kc=kc, c0=c0, cn=cn, i=i: e.tensor_copy(out=dst[:, kc, c0:c0 + cn], in_=stage[i][:, 0:cn]),
                          r=[sk], w=[dkey])
                    else:
                        A("act", lambda e, kc=kc, c0=c0, cn=cn, i=i: e.activation(out=dst[:, kc, c0:c0 + cn], in_=stage[i][:, 0:cn], func=AF.Copy),
                          r=[sk], w=[dkey])

        def sweep(xsrc, ntok, wk, per_chunk):
            xb = [wk.alloc([128, D], F32) for _ in range(2)]
            junk = wk.alloc([128, D], BF16)
            hnb = [wk.alloc([128, D], BF16) for _ in range(2)]
            ssq = [wk.alloc([128, 1], F32) for _ in range(2)]
            rs = [wk.alloc([128, 1], F32) for _ in range(2)]
            hnT = [wk.alloc([128, KC, CH], BF16) for _ in range(2)]
            tno = 0
            for ci, k0 in enumerate(range(0, ntok, CH)):
                n = min(CH, ntok - k0)
                hk = ("hnT", ci % 2)
                for tt in range(n // 128):
                    i = tno % 2
                    r0 = k0 + tt * 128
                    A("sp", lambda e, i=i, r0=r0: e.dma_start(out=xb[i], in_=xsrc[r0:r0 + 128, :]), w=[("xb", i)], dma=True)
                    A("act", lambda e, i=i: e.activation(out=junk, in_=xb[i], func=AF.Square, accum_out=ssq[i]),
                      r=[("xb", i)], w=[("ssq", i), "junk"])
                    A("act", lambda e, i=i: e.activation(out=rs[i], in_=ssq[i], func=AF.Sqrt, scale=1.0 / D, bias=EPS),
                      r=[("ssq", i)], w=[("rs", i)])
                    A("dve", lambda e, i=i: e.reciprocal(out=rs[i], in_=rs[i]), r=[("rs", i)], w=[("rs", i)])
                    A("dve", lambda e, i=i: e.scalar_tensor_tensor(out=hnb[i], in0=xb[i], scalar=rs[i][:, 0:1], in1=g1rep,
                                                                   op0=ALU.mult, op1=ALU.mult),
                      r=[("xb", i), ("rs", i), "g1rep"], w=[("hnb", i)])
                    pb = tno % 2
                    for kc in range(KC):
                        A("pe", lambda e, i=i, kc=kc, pb=pb: e.transpose(out=bank_bf(pb, KC * 128)[:, kc * 128:(kc + 1) * 128],
                                                                        in_=hnb[i][:, kc * 128:(kc + 1) * 128], identity=ident),
                          r=[("hnb", i), "ident"], w=[("ps", pb)])
                    A("dve", lambda e, ci=ci, tt=tt, pb=pb: e.tensor_copy(
                        out=hnT[ci % 2][:, :, tt * 128:(tt + 1) * 128],
                        in_=bank_bf(pb, KC * 128).rearrange("p (a b) -> p a b", a=KC)), r=[("ps", pb)], w=[hk])
                    tno += 1
                per_chunk(ci, k0, n, hnT[ci % 2], hk)

        def proj_fm(wt, wcol, hT_, hk, c_lo, nq, pbank, wkey):
            for kc in range(KC):
                A("pe", lambda e, kc=kc: e.matmul(bank(pbank, nq), lhsT=wt[:, kc, wcol:wcol + 128], rhs=hT_[:, kc, c_lo:c_lo + nq],
                                                  start=(kc == 0), stop=(kc == KC - 1)), r=[hk, wkey], w=[("ps", pbank)])

        class NormWork:
            def __init__(self, wk):
                self.sqb = [wk.alloc([128, CH], BF16) for _ in range(2)]
                self.rs = [wk.alloc([128, CH], F32) for _ in range(2)]
                self.t1 = [wk.alloc([128, CH], F32) for _ in range(2)]
                self.t2 = [wk.alloc([128, CH], F32) for _ in range(2)]
                self.n = 0

        def qknorm(nw, pq, pqs, nq, gcol, gscol, cos, sin, roper, out_ap, okey):
            i = nw.n % 2
            nw.n += 1
            sqb, rs, t1, t2 = nw.sqb[i], nw.rs[i], nw.t1[i], nw.t2[i]
            A("act", lambda e: e.activation(out=sqb[:, 0:nq], in_=bank(pq, nq), func=AF.Square), r=[("ps", pq)], w=[("sqb", i)])
            A("pe", lambda e: e.matmul(bank(6, nq), lhsT=bones, rhs=sqb[:, 0:nq], start=True, stop=True),
              r=[("sqb", i), "bones"], w=[("ps", 6)])
            A("act", lambda e: e.activation(out=rs[:, 0:nq], in_=bank(6, nq), func=AF.Sqrt, scale=1.0, bias=64 * EPS),
              r=[("ps", 6)], w=[("nrs", i)])
            A("dve", lambda e: e.reciprocal(out=rs[:, 0:nq], in_=rs[:, 0:nq]), r=[("nrs", i)], w=[("nrs", i)])
            if pqs is None:
                A("dve", lambda e: e.scalar_tensor_tensor(out=out_ap, in0=bank(pq, nq), scalar=gq[:, gcol:gcol + 1], in1=rs[:, 0:nq],
                                                          op0=ALU.mult, op1=ALU.mult), r=[("ps", pq), ("nrs", i), "gq"], w=[okey])
                return
            A("dve", lambda e: e.scalar_tensor_tensor(out=t1[:, 0:nq], in0=bank(pq, nq), scalar=gq[:, gcol:gcol + 1], in1=rs[:, 0:nq],
                                                      op0=ALU.mult, op1=ALU.mult), r=[("ps", pq), ("nrs", i), "gq"], w=[("t1", i)])
            A("dve", lambda e: e.scalar_tensor_tensor(out=t2[:, 0:nq], in0=bank(pqs, nq), scalar=gq[:, gscol:gscol + 1], in1=rs[:, 0:nq],
                                                      op0=ALU.mult, op1=ALU.mult), r=[("ps", pqs), ("nrs", i), "gq"], w=[("t2", i)])
            A("pool", lambda e: e.tensor_tensor(out=t1[:, 0:nq], in0=t1[:, 0:nq], in1=cos, op=ALU.mult), r=[("t1", i)] + roper, w=[("t1", i)])
            A("pool", lambda e: e.tensor_tensor(out=t2[:, 0:nq], in0=t2[:, 0:nq], in1=sin, op=ALU.mult), r=[("t2", i)] + roper, w=[("t2", i)])
            A("pool", lambda e: e.tensor_tensor(out=out_ap, in0=t1[:, 0:nq], in1=t2[:, 0:nq], op=ALU.add),
              r=[("t1", i), ("t2", i)], w=[okey])

        class PostWork:
            def __init__(self, wk, ggrep, goff):
                self.rd = [wk.alloc([128, 8], F32) for _ in range(2)]
                self.outf = [wk.alloc([128, 512], F32) for _ in range(2)]
                self.junk = wk.alloc([128, 512], BF16)
                self.ssq = [wk.alloc([128, 1], F32) for _ in range(2)]
                self.rs = [wk.alloc([128, 1], F32) for _ in range(2)]
                self.mixb = [wk.alloc([128, 512], BF16) for _ in range(2)]
                self.ggrep = ggrep
                self.goff = goff
                self.n = 0

        def mixer_post(pw, obank0, okeys, sz, e0, mixT, mkey, tbank):
            i = pw.n % 2
            pw.n += 1
            rd, outf, mixb = pw.rd[i], pw.outf[i], pw.mixb[i]
            for g in range(2):
                ov = bank(obank0 + g, 260).rearrange("p (a b) -> p a b", a=4)
                A("dve", lambda e, g=g, ov=ov: e.reciprocal(out=rd[0:sz, g * 4:(g + 1) * 4], in_=ov[0:sz, :, 64]),
                  r=[okeys[g]], w=[("rd", i, g)])
                for j in range(4):
                    h = g * 4 + j
                    A("act", lambda e, h=h, j=j, ov=ov: e.activation(out=outf[0:sz, h * 64:(h + 1) * 64], in_=ov[0:sz, j, 0:64],
                                                                    func=AF.Copy, scale=rd[0:sz, h:h + 1]),
                      r=[okeys[g], ("rd", i, g)], w=[("outf", i)])
            A("act", lambda e: e.activation(out=pw.junk[0:sz, :], in_=outf[0:sz, :], func=AF.Square, accum_out=pw.ssq[i][0:sz, :]),
              r=[("outf", i)], w=[("pssq", i), "pjunk"])
            A("act", lambda e: e.activation(out=pw.rs[i][0:sz, :], in_=pw.ssq[i][0:sz, :], func=AF.Sqrt, scale=1.0 / 512, bias=EPS),
              r=[("pssq", i)], w=[("prs", i)])
            A("dve", lambda e: e.reciprocal(out=pw.rs[i][0:sz, :], in_=pw.rs[i][0:sz, :]), r=[("prs", i)], w=[("prs", i)])
            A("dve", lambda e: e.scalar_tensor_tensor(out=mixb[0:sz, :], in0=outf[0:sz, :], scalar=pw.rs[i][0:sz, 0:1],
                                                      in1=pw.ggrep[0:sz, pw.goff:pw.goff + 512], op0=ALU.mult, op1=ALU.mult),
              r=[("outf", i), ("prs", i), "ggrep"], w=[("mixb", i)])
            tv = bank_bf(tbank, 512).rearrange("p (a b) -> p a b", a=4)
            for c in range(4):
                A("pe", lambda e, c=c: e.transpose(out=tv[:, c, 0:sz], in_=mixb[0:sz, c * 128:(c + 1) * 128], identity=ident[0:sz, 0:sz]),
                  r=[("mixb", i), "ident"], w=[("ps", tbank)])
            A("dve", lambda e: e.tensor_copy(out=mixT[:, :, e0:e0 + sz], in_=tv[:, :, 0:sz]), r=[("ps", tbank)], w=[mkey])

        qtiles = [(0, 1)] + [(1 + 128 * i, 128) for i in range(NT // 128)] + [(NT + 1, 1)]

        KBT = M.alloc([128, 4, WIN], BF16)
        VB = M.alloc([128, NWT, 8, 65], BF16)
        QBT = M.alloc([128, 4, E], BF16)
        mark_w = M.off
        WinB = M.alloc([128, KC, 2560], BF16)
        load_cast(WinB, w_inB, D, 2560, "WinB", stage)
        vtmp = M.alloc([128, NWT, 8], F32)
        A("sp", lambda e: e.dma_start(out=vtmp, in_=vones_d), w=["vtmp"], dma=True)
        A("pool", lambda e: e.tensor_copy(out=VB[:, :, :, 64], in_=vtmp), r=["vtmp"], w=["VBones"])
        nw = NormWork(M)
        ropebW = [[M.alloc([128, CH], F32) for _ in range(2)] for _ in range(2)]
        vcnt = [0]

        def chunk_W(ci, k0, n, hT_, hk):
            for p in range(4):
                pb = 2 + (p % 2) * 2
                proj_fm(WinB, 1536 + p * 128, hT_, hk, 0, n, pb, "WinB")
                qknorm(nw, pb, None, n, 5, None, None, None, None, KBT[:, p, k0:k0 + n], ("KBT", ci))
            for tt in range(n // 128):
                t = (k0 // 128) + tt
                for kc in range(KC):
                    A("pe", lambda e, kc=kc, tt=tt: e.matmul(bank(7, 512), lhsT=hT_[:, kc, tt * 128:(tt + 1) * 128], rhs=WinB[:, kc, 2048:2560],
                                                             start=(kc == 0), stop=(kc == KC - 1)), r=[hk, "WinB"], w=[("ps", 7)])
                A("act", lambda e, t=t: e.activation(out=VB[:, t, :, 0:64], in_=bank(7, 512).rearrange("p (a b) -> p a b", a=8), func=AF.Copy),
                  r=[("ps", 7)], w=[("VB", t)])
            c_lo = max(k0, HALO - 1)
            c_hi = min(k0 + n, HALO - 1 + E)
            if c_lo < c_hi:
                nq = c_hi - c_lo
                e_lo = c_lo - (HALO - 1)
                off = c_lo - k0
                ri = ci % 2
                rk = ("ropeq", ri)
                rb = ropebW[ri]
                A("sp", lambda e: e.dma_start(out=rb[0][:, 0:nq], in_=ropeq_d[0, :, e_lo:e_lo + nq]), w=[rk], dma=True)
                A("sp", lambda e: e.dma_start(out=rb[1][:, 0:nq], in_=ropeq_d[1, :, e_lo:e_lo + nq]), w=[rk], dma=True)
                for p in range(4):
                    pb = 2 + (p % 2) * 2
                    proj_fm(WinB, 1024 + p * 128, hT_, hk, off, nq, pb, "WinB")
                    qknorm(nw, pb, None, nq, 4, None, None, None, None, QBT[:, p, e_lo:e_lo + nq], "QBT")
                for p in range(4):
                    pb = 2 + (p % 2) * 2
                    proj_fm(WinB, p * 128, hT_, hk, off, nq, pb, "WinB")
                    proj_fm(WinB, 512 + p * 128, hT_, hk, off, nq, pb + 1, "WinB")
                    qknorm(nw, pb, pb + 1, nq, 0, 1, rb[0][:, 0:nq], rb[1][:, 0:nq], [rk],
                           QAT[:, p, e_lo:e_lo + nq], "QAT")

        sweep(xwin, WIN, M, chunk_W)
        S.barrier()

        M.off = mark_w
        mixTB = M.alloc([128, 4, E], BF16)
        mixTB_end = M.off
        maskB = M.alloc([128, 8, MW], BF16)
        A("sp", lambda e: e.dma_start(out=maskB, in_=maskB_d), w=["maskB"], dma=True)
        ggrep = M.alloc([128, 1024], F32)
        A("sp", lambda e: e.dma_start(out=ggrep, in_=gg_d.partition_broadcast(128)), w=["ggrep"], dma=True)
        qz = [M.alloc([128, 8, 128], BF16) for _ in range(2)]
        Eb = [M.alloc([128, 8, 128], BF16) for _ in range(2)]
        PT = [M.alloc([128, 8, 128], BF16) for _ in range(2)]
        for i in range(2):
            A("pool", lambda e, i=i: e.memset(qz[i], 0.0), w=[("qz", i)])
        pwB = PostWork(M, ggrep, 512)
        anyKBT = [("KBT", ci) for ci in range((WIN + CH - 1) // CH)]
        def tile_B(qi, e0, sz):
            zi = qi % 2
            kq0 = e0 + HALO - 1
            kt_lo = max(0, (kq0 - 1024) // 128)
            kt_hi = min(NWT - 1, (kq0 + sz - 1 + 1024) // 128)
            qv = qz[zi].rearrange("p (a b) c -> p a b c", b=2)
            A("pool", lambda e, qv=qv, e0=e0, sz=sz: e.tensor_copy(out=qv[0:64, :, 0, 0:sz], in_=QBT[0:64, :, e0:e0 + sz]),
              r=["QBT"], w=[("qz", zi)])
            A("pool", lambda e, qv=qv, e0=e0, sz=sz: e.tensor_copy(out=qv[64:128, :, 1, 0:sz], in_=QBT[64:128, :, e0:e0 + sz]),
              r=["QBT"], w=[("qz", zi)])
            ob = 4
            okeys = [("ps", 4), ("ps", 5)]
            kts = list(range(kt_lo, kt_hi + 1))

            def qk(n_, kt):
                sb_ = (n_ % 2) * 2
                sv = psum[:, sb_ * 512: sb_ * 512 + 1024].rearrange("p (a b) -> p a b", a=8)
                for h in range(8):
                    A("pe", lambda e, h=h, kt=kt, sv=sv: e.matmul(sv[:, h, 0:sz], lhsT=KBT[:, h // 2, kt * 128:(kt + 1) * 128],
                                                                  rhs=qz[zi][:, h, 0:sz], start=True, stop=True),
                      r=[("qz", zi)] + anyKBT[(kt * 128) // CH:(kt * 128) // CH + 1], w=[("S", n_ % 2)])

            def ex(n_, kt):
                sb_ = (n_ % 2) * 2
                sv = psum[:, sb_ * 512: sb_ * 512 + 1024].rearrange("p (a b) -> p a b", a=8)
                bi = n_ % 2
                A("act", lambda e, sv=sv, bi=bi: e.activation(out=Eb[bi][:, :, 0:sz], in_=sv[:, :, 0:sz], func=AF.Exp, scale=8.0),
                  r=[("S", n_ % 2)], w=[("Eb", bi)])
                c0 = kq0 - 128 * kt + C0
                eng = ("dve", "pool")[n_ % 2]
                A(eng, lambda e, bi=bi, c0=c0: e.tensor_tensor(out=PT[bi][:, :, 0:sz], in0=Eb[bi][:, :, 0:sz],
                                                               in1=maskB[:, :, c0:c0 + sz], op=ALU.mult),
                  r=[("Eb", bi), "maskB"], w=[("PT", bi)])

            def pv(n_, kt):
                bi = n_ % 2
                for h in range(8):
                    g, j = h // 4, h % 4
                    A("pe", lambda e, h=h, g=g, j=j, kt=kt, bi=bi, n_=n_: e.matmul(
                        bank(ob + g, 65, j * 65)[0:sz, :], lhsT=PT[bi][:, h, 0:sz], rhs=VB[:, kt, h, :],
                        start=(n_ == 0 and j == 0), stop=(n_ == len(kts) - 1 and j == 3)),
                      r=[("PT", bi), ("VB", kt), "VBones"], w=[okeys[g]])

            for n_, kt in enumerate(kts):
                qk(n_, kt)
                if n_ >= 1:
                    pv(n_ - 1, kts[n_ - 1])
                ex(n_, kt)
            pv(len(kts) - 1, kts[-1])
            mixer_post(pwB, ob, okeys, sz, e0, mixTB, "mixTB", 6)

        for qi, (e0, sz) in enumerate(qtiles):
            tile_B(qi, e0, sz)
        S.barrier()

        M.off = mark_persist
        hT = M.alloc([128, KC, E], BF16)
        KAT = M.alloc([128, SEQ], BF16)
        VA = M.alloc([128, NKT, 2, 65], BF16)
        mixTA = M.alloc([128, 4, E], BF16)
        M.off = max(M.off, mixTB_end)
        mark_a = M.off
        WinA = M.alloc([128, KC, 384], BF16)
        load_cast(WinA, w_inA, D, 384, "WinA", stage)
        A("pool", lambda e: e.memset(VA[:, :, :, 64], 1.0), w=["VAones"])
        nw = NormWork(M)
        ropebA = [[M.alloc([128, CH], F32) for _ in range(2)] for _ in range(2)]

        def chunk_A(ci, k0, n, hT_, hk):
            ri = ci % 2
            rk = ("ropek", ri)
            rb = ropebA[ri]
            A("sp", lambda e: e.dma_start(out=rb[0][:, 0:n], in_=ropek_d[0, :, k0:k0 + n]), w=[rk], dma=True)
            A("sp", lambda e: e.dma_start(out=rb[1][:, 0:n], in_=ropek_d[1, :, k0:k0 + n]), w=[rk], dma=True)
            pb = 2 + (ci % 2) * 2
            proj_fm(WinA, 0, hT_, hk, 0, n, pb, "WinA")
            proj_fm(WinA, 128, hT_, hk, 0, n, pb + 1, "WinA")
            qknorm(nw, pb, pb + 1, n, 2, 3, rb[0][:, 0:n], rb[1][:, 0:n], [rk], KAT[:, k0:k0 + n], ("KAT", ci))
            for tt in range(n // 128):
                t = (k0 // 128) + tt
                for kc in range(KC):
                    A("pe", lambda e, kc=kc, tt=tt: e.matmul(bank(7, 128), lhsT=hT_[:, kc, tt * 128:(tt + 1) * 128], rhs=WinA[:, kc, 256:384],
                                                             start=(kc == 0), stop=(kc == KC - 1)), r=[hk, "WinA"], w=[("ps", 7)])
                A("act", lambda e, t=t: e.activation(out=VA[:, t, :, 0:64], in_=bank(7, 128).rearrange("p (a b) -> p a b", a=2), func=AF.Copy),
                  r=[("ps", 7)], w=[("VA", t)])

        sweep(xfull, SEQ, M, chunk_A)
        S.barrier()

        M.off = mark_a
        Wout = M.alloc([128, 8, D], BF16)
        load_cast(Wout, w_out_d, 1024, D, "Wout", stage)
        ggrepA = M.alloc([128, 1024], F32)
        A("sp", lambda e: e.dma_start(out=ggrepA, in_=gg_d.partition_broadcast(128)), w=["ggrep"], dma=True)
        g2rep = M.alloc([128, D], F32)
        A("sp", lambda e: e.dma_start(out=g2rep, in_=g2_d.partition_broadcast(128)), w=["g2rep"], dma=True)
        qzA = [M.alloc([128, 2, 4, 128], BF16) for _ in range(2)]
        PTA = [M.alloc([128, 2, 4, 128], BF16) for _ in range(2)]
        for i in range(2):
            A("pool", lambda e, i=i: e.memset(qzA[i], 0.0), w=[("qzA", i)])
        pwA = PostWork(M, ggrepA, 0)
        xr = [M.alloc([128, D], F32) for _ in range(2)]
        xm = [M.alloc([128, D], F32) for _ in range(2)]
        hb = [M.alloc([128, D], BF16) for _ in range(2)]
        junk2 = M.alloc([128, D], BF16)
        ssq2 = [M.alloc([128, 1], F32) for _ in range(2)]
        rs2 = [M.alloc([128, 1], F32) for _ in range(2)]
        anyKAT = [("KAT", ci) for ci in range(SEQ // CH)]
        def tile_A(qi, e0, sz):
            zi = qi % 2
            kq0 = e0 + HALO - 1
            A("pool", lambda e, e0=e0, sz=sz, zi=zi: e.tensor_copy(out=qzA[zi][0:64, 0, :, 0:sz], in_=QAT[0:64, :, e0:e0 + sz]),
              r=["QAT"], w=[("qzA", zi)])
            A("pool", lambda e, e0=e0, sz=sz, zi=zi: e.tensor_copy(out=qzA[zi][64:128, 1, :, 0:sz], in_=QAT[64:128, :, e0:e0 + sz]),
              r=["QAT"], w=[("qzA", zi)])
            A("sp", lambda e, zi=zi, kq0=kq0, sz=sz: e.dma_start(out=xr[zi][0:sz, :], in_=xwin[kq0:kq0 + sz, :]), w=[("xr", zi)], dma=True)
            ob = 4
            okeys = [("ps", 4), ("ps", 5)]

            def qkA(kt):
                sb_ = (kt % 2) * 2
                sv = psum[:, sb_ * 512: sb_ * 512 + 1024].rearrange("p (g a b) -> p g a b", g=2, a=4)
                for g in range(2):
                    A("pe", lambda e, g=g, kt=kt, sv=sv: e.matmul(sv[:, g, :, 0:sz], lhsT=KAT[:, kt * 128:(kt + 1) * 128],
                                                                  rhs=qzA[zi][:, g, :, 0:sz], start=True, stop=True),
                      r=[("qzA", zi), anyKAT[(kt * 128) // CH]], w=[("S", kt % 2)])

            def exA(kt):
                sb_ = (kt % 2) * 2
                sv = psum[:, sb_ * 512: sb_ * 512 + 1024].rearrange("p (g a b) -> p g a b", g=2, a=4)
                bi = kt % 2
                A("act", lambda e, sv=sv, bi=bi: e.activation(out=PTA[bi][:, :, :, 0:sz], in_=sv[:, :, :, 0:sz], func=AF.Exp, scale=8.0),
                  r=[("S", kt % 2)], w=[("PTA", bi)])

            def pvA(kt):
                bi = kt % 2
                for g in range(2):
                    for j in range(4):
                        A("pe", lambda e, g=g, j=j, kt=kt, bi=bi: e.matmul(
                            bank(ob + g, 65, j * 65)[0:sz, :], lhsT=PTA[bi][:, g, j, 0:sz], rhs=VA[:, kt, g, :],
                            start=(kt == 0 and j == 0), stop=(kt == NKT - 1 and j == 3)),
                          r=[("PTA", bi), ("VA", kt), "VAones"], w=[okeys[g]])

            for kt in range(NKT):
                qkA(kt)
                if kt >= 1:
                    pvA(kt - 1)
                exA(kt)
            pvA(NKT - 1)
            mixer_post(pwA, ob, okeys, sz, e0, mixTA, "mixTA", 6)
            for nb_ in range(NB):
                for kc in range(8):
                    src = mixTA[:, kc] if kc < 4 else mixTB[:, kc - 4]
                    A("pe", lambda e, nb_=nb_, kc=kc, src=src: e.matmul(bank(nb_, 512)[0:sz, :], lhsT=src[:, e0:e0 + sz],
                                                                        rhs=Wout[:, kc, nb_ * 512:(nb_ + 1) * 512],
                                                                        start=(kc == 0), stop=(kc == 7)),
                      r=["mixTA", "mixTB", "Wout"], w=[("S", nb_ // 2)] if NB > 1 else [("S", 0)])
            ykeys = [("S", nb_ // 2) for nb_ in range(NB)] if NB > 1 else [("S", 0)]
            for nb_ in range(NB):
                A("dve", lambda e, nb_=nb_, zi=zi: e.tensor_tensor(out=xm[zi][0:sz, nb_ * 512:(nb_ + 1) * 512], in0=bank(nb_, 512)[0:sz, :],
                                                                   in1=xr[zi][0:sz, nb_ * 512:(nb_ + 1) * 512], op=ALU.add),
                  r=[ykeys[nb_], ("xr", zi)], w=[("xm", zi)])
            A("sp", lambda e, zi=zi, e0=e0, sz=sz: e.dma_start(out=xmid_d[e0:e0 + sz, :], in_=xm[zi][0:sz, :]),
              r=[("xm", zi)], w=[("xmid_d", e0)], dma=True)
            A("act", lambda e, zi=zi: e.activation(out=junk2[0:sz, :], in_=xm[zi][0:sz, :], func=AF.Square, accum_out=ssq2[zi][0:sz, :]),
              r=[("xm", zi)], w=[("ssq2", zi), "junk2"])
            A("act", lambda e, zi=zi: e.activation(out=rs2[zi][0:sz, :], in_=ssq2[zi][0:sz, :], func=AF.Sqrt, scale=1.0 / D, bias=EPS),
              r=[("ssq2", zi)], w=[("rs2", zi)])
            A("dve", lambda e, zi=zi: e.reciprocal(out=rs2[zi][0:sz, :], in_=rs2[zi][0:sz, :]), r=[("rs2", zi)], w=[("rs2", zi)])
            A("dve", lambda e, zi=zi: e.scalar_tensor_tensor(out=hb[zi][0:sz, :], in0=xm[zi][0:sz, :], scalar=rs2[zi][0:sz, 0:1],
                                                             in1=g2rep[0:sz, :], op0=ALU.mult, op1=ALU.mult),
              r=[("xm", zi), ("rs2", zi), "g2rep"], w=[("hb", zi)])
            tv = bank_bf(7, KC * 128).rearrange("p (a b) -> p a b", a=KC)
            for kc in range(KC):
                A("pe", lambda e, kc=kc, zi=zi: e.transpose(out=tv[:, kc, 0:sz], in_=hb[zi][0:sz, kc * 128:(kc + 1) * 128],
                                                            identity=ident[0:sz, 0:sz]), r=[("hb", zi), "ident"], w=[("ps", 7)])
            A("dve", lambda e, e0=e0, sz=sz: e.tensor_copy(out=hT[:, :, e0:e0 + sz], in_=tv[:, :, 0:sz]), r=[("ps", 7)], w=["hT"])

        for qi, (e0, sz) in enumerate(qtiles):
            tile_A(qi, e0, sz)
        A("act", lambda e: e.activation(out=hT[:, :, 0:1], in_=hT[:, :, 0:1], func=AF.Copy, scale=halo_m[:, 0:1]),
          r=["hT", "halo_m"], w=["hT"])
        A("act", lambda e: e.activation(out=hT[:, :, E - 1:E], in_=hT[:, :, E - 1:E], func=AF.Copy, scale=halo_m[:, 1:2]),
          r=["hT", "halo_m"], w=["hT"])
        S.barrier()

        M.off = mark_persist + (KC * E * 2 + 31) // 32 * 32
        actT = M.alloc([128, NJ, HN], BF16)
        Wd = M.alloc([128, NJ, D], BF16)
        convw = M.alloc([128, 2 * NJ, 3], F32)
        convb = M.alloc([128, 2 * NJ], F32)
        A("sp", lambda e: e.dma_start(out=convw, in_=convw_d), w=["convw"], dma=True)
        A("sp", lambda e: e.dma_start(out=convb, in_=convb_d), w=["convb"], dma=True)
        load_cast(Wd, w_down_d, DFF, D, "Wd", stage)
        wst = [M.alloc([128, KC, 256], F32) for _ in range(2)]
        wub = [M.alloc([128, KC, 256], BF16) for _ in range(2)]
        cg = [M.alloc([128, HN], F32) for _ in range(2)]
        cv = [M.alloc([128, HN], F32) for _ in range(2)]
        tb = M.alloc([128, HN], F32)
        xmF = [M.alloc([128, D], F32) for _ in range(2)]
        yt = [M.alloc([128, D], F32) for _ in range(2)]
        W = HN + 2
        pieces = [(c, min(512, W - c)) for c in range(0, W, 512)]
        wup_v = w_up_d.rearrange("(kc p) c -> p kc c", p=128)

        def ffn_j(hf, j, it):
            eb = HN * hf
            wi = it % 2
            A("sp", lambda e: e.dma_start(out=wst[wi][:, :, 0:128], in_=wup_v[:, :, j * 128:(j + 1) * 128]),
              w=[("wst", wi)], dma=True)
            A("sp", lambda e: e.dma_start(out=wst[wi][:, :, 128:256], in_=wup_v[:, :, DFF + j * 128:DFF + (j + 1) * 128]),
              w=[("wst", wi)], dma=True)
            A("pool", lambda e: e.tensor_copy(out=wub[wi], in_=wst[wi]), r=[("wst", wi)], w=[("wub", wi)])
            for half, b0 in ((0, 0), (1, 3)):
                for pi, (c, cn) in enumerate(pieces):
                    for kc in range(KC):
                        A("pe", lambda e, kc=kc, c=c, cn=cn, b0=b0, pi=pi, half=half: e.matmul(
                            bank(b0 + pi, cn), lhsT=wub[wi][:, kc, half * 128:(half + 1) * 128], rhs=hT[:, kc, eb + c:eb + c + cn],
                            start=(kc == 0), stop=(kc == KC - 1)), r=[("wub", wi), "hT"], w=[("G", half)])
            ci_ = j % 2
            for half, b0, cbuf in ((0, 0, cg[ci_]), (1, 3, cv[ci_])):
                jj = j + half * NJ
                P0 = psum[:, b0 * 512: b0 * 512 + W]
                ck = ("c", half, ci_)
                A("act", lambda e, P0=P0, cbuf=cbuf, jj=jj: e.activation(out=cbuf, in_=P0[:, 1:HN + 1], func=AF.Identity,
                                                                        scale=convw[:, jj, 1:2], bias=convb[:, jj:jj + 1]),
                  r=[("G", half), "convw", "convb"], w=[ck])
                A("dve", lambda e, P0=P0, cbuf=cbuf, jj=jj: e.scalar_tensor_tensor(out=cbuf, in0=P0[:, 0:HN], scalar=convw[:, jj, 0:1],
                                                                                  in1=cbuf, op0=ALU.mult, op1=ALU.add),
                  r=[("G", half), "convw", ck], w=[ck])
                A("dve", lambda e, P0=P0, cbuf=cbuf, jj=jj: e.scalar_tensor_tensor(out=cbuf, in0=P0[:, 2:HN + 2], scalar=convw[:, jj, 2:3],
                                                                                  in1=cbuf, op0=ALU.mult, op1=ALU.add),
                  r=[("G", half), "convw", ck], w=[ck])
            cgb, cvb = cg[ci_], cv[ci_]
            gk, vk = ("c", 0, ci_), ("c", 1, ci_)
            A("act", lambda e: e.activation(out=tb, in_=cgb, func=AF.Square), r=[gk], w=["tb"])
            A("act", lambda e: e.activation(out=tb, in_=tb, func=AF.Identity, scale=0.0713548162726, bias=1.5957691216057308),
              r=["tb"], w=["tb"])
            A("pool", lambda e: e.tensor_tensor(out=tb, in0=tb, in1=cgb, op=ALU.mult), r=["tb", gk], w=["tb"])
            A("act", lambda e: e.activation(out=tb, in_=tb, func=AF.Sigmoid), r=["tb"], w=["tb"])
            A("pool", lambda e: e.tensor_tensor(out=cvb, in0=cgb, in1=cvb, op=ALU.mult), r=[gk, vk], w=[vk])
            A("pool", lambda e: e.tensor_tensor(out=actT[:, j, :], in0=cvb, in1=tb, op=ALU.mult), r=[vk, "tb"], w=["actT"])

        def down_tile(hf, ti):
            eb = HN * hf
            zi = ti % 2
            e0 = 1 + eb + 128 * ti
            A("sp", lambda e: e.dma_start(out=xmF[zi], in_=xmid_d[e0:e0 + 128, :]),
              r=[("xmid_d", e0)], w=[("xmF", zi)], dma=True)
            for nb_ in range(NB):
                for j in range(NJ):
                    A("pe", lambda e, nb_=nb_, j=j: e.matmul(bank(6 + nb_ % 2, 512), lhsT=actT[:, j, ti * 128:(ti + 1) * 128],
                                                             rhs=Wd[:, j, nb_ * 512:(nb_ + 1) * 512],
                                                             start=(j == 0), stop=(j == NJ - 1)),
                      r=["actT", "Wd"], w=[("ps", 6 + nb_ % 2)])
                A("dve", lambda e, nb_=nb_: e.tensor_tensor(out=yt[zi][:, nb_ * 512:(nb_ + 1) * 512], in0=bank(6 + nb_ % 2, 512),
                                                            in1=xmF[zi][:, nb_ * 512:(nb_ + 1) * 512], op=ALU.add),
                  r=[("ps", 6 + nb_ % 2), ("xmF", zi)], w=[("yt", zi)])
            out_ops.append(A("sp", lambda e: e.dma_start(out=y_d[e0 - 1:e0 - 1 + 128, :], in_=yt[zi]),
                             r=[("yt", zi)], dma=True))

        it = 0
        for hf in range(2):
            for j in range(NJ):
                ffn_j(hf, j, it)
                it += 1
            for ti in range(HN // 128):
                down_tile(hf, ti)
        S.emit(nc, st, final_wait_ops=out_ops)
    return nc


def _swap_idx():
    idx = np.arange(64)
    sw = idx.copy()
    sw[0:16], sw[16:32], sw[32:48], sw[48:64] = idx[16:32], idx[0:16], idx[48:64], idx[32:48]
    return sw


def _rope_tables(pos):
    pos = np.asarray(pos, dtype=np.float64)
    valid = pos >= 0
    p = np.where(valid, pos, 0)
    row = np.floor(p / 64).astype(np.float32)
    col = (p % 64).astype(np.float32)
    inv = (np.float32(10000.0) ** (-np.arange(0, 32, 2, dtype=np.float32) / np.float32(32))).astype(np.float32)
    ar = (row[:, None] * inv[None, :]).astype(np.float32)
    ac = (col[:, None] * inv[None, :]).astype(np.float32)
    cos = np.concatenate([np.cos(ar), np.cos(ar), np.cos(ac), np.cos(ac)], axis=1).T
    sin = np.concatenate([-np.sin(ar), np.sin(ar), -np.sin(ac), np.sin(ac)], axis=1).T
    cos = np.concatenate([cos, cos], axis=0)
    sin = np.concatenate([sin, sin], axis=0)
    return np.ascontiguousarray(np.stack([cos, sin]).astype(np.float32))


def _maskB():
    ik = np.arange(128)[:, None]
    c = np.arange(MW)[None, :]
    dl = (c - C0 - ik).astype(np.int64)
    ad = np.abs(dl)
    m = (ad <= 64).astype(np.float64) + ((ad <= 256) & (dl % 4 == 0)) + ((ad <= 1024) & (dl % 16 == 0))
    out = np.zeros((128, 8, MW), np.float32)
    for h in range(8):
        slope = 2.0 ** (-8.0 * (h + 1) / 8)
        out[:, h, :] = m * np.exp(-slope * ad)
    return out.astype(ml_dtypes.bfloat16)


_CACHE = {}


def _run(x, norm1_g, w_in, qa_norm_g, ka_norm_g, qb_norm_g, kb_norm_g, outa_norm_g, outb_norm_g,
         w_out, norm2_g, w_up, conv_w, conv_b, w_down, runner=None):
    f = lambda a: np.ascontiguousarray(np.asarray(a, dtype=np.float32))
    x, w_in, w_out, w_up, w_down = f(x), f(w_in), f(w_out), f(w_up), f(w_down)
    B, SEQ, D = x.shape
    DFF = w_down.shape[0]
    NT = SEQ // 4
    NJ = DFF // 128
    WIN = NT + 2 * HALO
    NWT = WIN // 128
    E = NT + 2
    sw = _swap_idx()
    qa = w_in[:, 0:512].reshape(D, 8, 64)
    pair = np.stack([np.concatenate([qa[:, p], qa[:, 4 + p]], axis=1) for p in range(4)], axis=1)
    pair_s = np.stack([np.concatenate([qa[:, p][:, sw], qa[:, 4 + p][:, sw]], axis=1) for p in range(4)], axis=1)
    ka = w_in[:, 512:640].reshape(D, 2, 64)
    ka_s = ka[:, :, sw]
    va = w_in[:, 640:768]
    qb, kb, vb = w_in[:, 768:1280], w_in[:, 1280:1792], w_in[:, 1792:2304]
    w_inB = np.ascontiguousarray(np.concatenate([pair.reshape(D, 512), pair_s.reshape(D, 512), qb, kb, vb], axis=1))
    w_inA = np.ascontiguousarray(np.concatenate([ka.reshape(D, 128), ka_s.reshape(D, 128), va], axis=1))
    d2 = lambda g: np.concatenate([f(g), f(g)])
    gq = np.ascontiguousarray(np.stack([d2(qa_norm_g), d2(f(qa_norm_g)[sw]), d2(ka_norm_g), d2(f(ka_norm_g)[sw]),
                                        d2(qb_norm_g), d2(kb_norm_g)], axis=1))
    gg = np.ascontiguousarray(np.concatenate([f(outa_norm_g), f(outb_norm_g)]))
    convw = np.ascontiguousarray(f(conv_w).T.reshape(2 * NJ, 128, 3).transpose(1, 0, 2))
    convb = np.ascontiguousarray(f(conv_b).reshape(2 * NJ, 128).T)
    ropek = _rope_tables(np.arange(SEQ))
    maskB = _maskB()
    shared = dict(w_inB=w_inB, w_inA=w_inA, gq=gq, g1=f(norm1_g), ropek=ropek, maskB=maskB, w_out=w_out, gg=gg,
                  g2=f(norm2_g), w_up=w_up, convw=convw, convb=convb, w_down=w_down)
    in_maps = []
    for c in range(8):
        b, qtr = c // 4, c % 4
        T0 = qtr * NT
        xpad = np.zeros((SEQ + 2 * HALO, D), np.float32)
        xpad[HALO:HALO + SEQ] = x[b]
        xwin = np.ascontiguousarray(xpad[T0:T0 + WIN])
        tpos = T0 - HALO + np.arange(WIN)
        valid = ((tpos >= 0) & (tpos < SEQ)).astype(np.float32)
        vones = np.ascontiguousarray(np.repeat(valid.reshape(NWT, 128).T[:, :, None], 8, axis=2))
        epos = T0 - 1 + np.arange(E)
        ropeq = _rope_tables(np.where((epos >= 0) & (epos < SEQ), epos, -1))
        halo_m = np.zeros((128, 2), np.float32)
        halo_m[:, 0] = 1.0 if T0 - 1 >= 0 else 0.0
        halo_m[:, 1] = 1.0 if T0 + NT < SEQ else 0.0
        m = dict(shared)
        m.update(xfull=np.ascontiguousarray(x[b]), xwin=xwin, vones=vones, ropeq=ropeq, halo_m=halo_m)
        in_maps.append(m)
    ck = (D, SEQ, NT, DFF)
    if ck not in _CACHE:
        _CACHE[ck] = build(D, SEQ, NT, DFF)
    nc = _CACHE[ck]
    if runner is not None:
        outs = runner(nc, in_maps)
    else:
        res = run_bass_kernel_spmd(nc, in_maps, core_ids=list(range(8)))
        outs = [r["y"] for r in res.results]
    y = np.zeros((B, SEQ, D), np.float32)
    for c in range(8):
        b, qtr = c // 4, c % 4
        y[b, qtr * NT:(qtr + 1) * NT] = np.asarray(outs[c]).reshape(NT, D)
    return y


def kernel(**inputs):
    return _run(**inputs)
```

```python
import math
from collections import defaultdict
from contextlib import ExitStack

import numpy as np
import ml_dtypes

import concourse.bass as bass
import concourse.mybir as mybir
from concourse.bass_utils import run_bass_kernel_spmd

F32 = mybir.dt.float32
BF16 = mybir.dt.bfloat16
AF = mybir.ActivationFunctionType
ALU = mybir.AluOpType

HALO = 1152
C0 = 1151
MW = 2430
EPS = 1e-6


class Op:
    __slots__ = ("eng", "fn", "reads", "writes", "dma", "deps", "sig", "needs_sig", "prev_dma")

    def __init__(self, eng, fn, reads, writes, dma):
        self.eng, self.fn, self.reads, self.writes, self.dma = eng, fn, reads, writes, dma
        self.deps = []
        self.sig = None
        self.needs_sig = False
        self.prev_dma = None


class Sched:
    def __init__(self, n_dma_sems=8):
        self.ops = []
        self.last_writer = {}
        self.readers = defaultdict(list)
        self.n_dma_sems = n_dma_sems
        self.last_on_eng = {}
        self.pending_barrier = {}
        self.outstanding_dma = []

    def barrier(self):
        deps = list(self.last_on_eng.values()) + list(self.outstanding_dma)
        self.outstanding_dma = []
        for e in ("pe", "act", "dve", "pool", "sp"):
            self.pending_barrier[e] = list(self.pending_barrier.get(e, [])) + deps

    def add(self, eng, fn, r=(), w=(), dma=False):
        op = Op(eng, fn, tuple(r), tuple(w), dma)
        deps = set()
        for b in op.reads:
            lw = self.last_writer.get(b)
            if lw is not None:
                deps.add(lw)
        for b in op.writes:
            lw = self.last_writer.get(b)
            if lw is not None:
                deps.add(lw)
            for rd in self.readers.get(b, ()):
                deps.add(rd)
        fdeps = []
        for d in deps:
            if d.eng == op.eng and not d.dma and not op.dma:
                if op.eng == "pe":
                    continue
                if op.eng != "pool" and not any(b in d.writes for b in op.reads):
                    continue
            fdeps.append(d)
        pb = self.pending_barrier.pop(eng, None)
        if pb:
            for d in pb:
                if d is not op and (d.dma or d.eng != eng):
                    fdeps.append(d)
        op.deps = fdeps
        for d in fdeps:
            d.needs_sig = True
        for b in op.reads:
            self.readers[b].append(op)
        for b in op.writes:
            self.last_writer[b] = op
            self.readers[b] = []
        self.ops.append(op)
        if dma:
            self.outstanding_dma.append(op)
        else:
            self.last_on_eng[eng] = op
        return op

    def emit(self, nc, stack, final_wait_ops=()):
        sems = {e: stack.enter_context(nc.semaphore("s_" + e)) for e in ("pe", "act", "dve", "pool")}
        dma_sems = [stack.enter_context(nc.semaphore("d_sp%d" % i)) for i in range(self.n_dma_sems)]
        cnt = defaultdict(int)
        rr = 0
        dcnt = defaultdict(int)
        dprev = {}
        for op in self.ops:
            if op.dma:
                i = rr % self.n_dma_sems
                rr += 1
                dcnt[i] += 1
                op.sig = (dma_sems[i], 16 * dcnt[i], 16)
                op.prev_dma = dprev.get(i)
                dprev[i] = op
            elif op.needs_sig:
                cnt[op.eng] += 1
                op.sig = (sems[op.eng], cnt[op.eng], 1)
        per_eng = defaultdict(list)
        for op in self.ops:
            per_eng[op.eng].append(op)
        block = stack.enter_context(nc.Block())

        def run(e, name):
            waited = {}

            def wait(sig):
                sem, val, _ = sig
                k = id(sem)
                if waited.get(k, 0) < val:
                    e.wait_ge(sem, val)
                    waited[k] = val

            for op in per_eng[name]:
                for d in op.deps:
                    wait(d.sig)
                if op.dma and op.prev_dma is not None:
                    wait(op.prev_dma.sig)
                ins = op.fn(e)
                if op.sig is not None:
                    ins.then_inc(op.sig[0], op.sig[2])
            if name == "sp":
                for op in final_wait_ops:
                    wait(op.sig)

        @block.tensor
        def _(e):
            run(e, "pe")

        @block.scalar
        def _(e):
            run(e, "act")

        @block.vector
        def _(e):
            run(e, "dve")

        @block.gpsimd
        def _(e):
            run(e, "pool")

        @block.sync
        def _(e):
            run(e, "sp")


def build(D, SEQ, NT, DFF):
    KC = D // 128
    NB = D // 512
    WIN = NT + 2 * HALO
    NWT = WIN // 128
    E = NT + 2
    NJ = DFF // 128
    NKT = SEQ // 128
    HN = NT // 2
    CH = 256

    nc = bass.Bass("TRN2", target_bir_lowering=False)

    def din(name, shape, dt=F32):
        return nc.dram_tensor(name, list(shape), dt, kind="ExternalInput").ap()

    xfull = din("xfull", [SEQ, D])
    xwin = din("xwin", [WIN, D])
    w_inB = din("w_inB", [D, 2560])
    w_inA = din("w_inA", [D, 384])
    gq_d = din("gq", [128, 6])
    g1_d = din("g1", [D])
    ropeq_d = din("ropeq", [2, 128, E])
    ropek_d = din("ropek", [2, 128, SEQ])
    maskB_d = din("maskB", [128, 8, MW], BF16)
    vones_d = din("vones", [128, NWT, 8])
    halo_d = din("halo_m", [128, 2])
    w_out_d = din("w_out", [1024, D])
    gg_d = din("gg", [1024])
    g2_d = din("g2", [D])
    w_up_d = din("w_up", [D, 2 * DFF])
    convw_d = din("convw", [128, 2 * NJ, 3])
    convb_d = din("convb", [128, 2 * NJ])
    w_down_d = din("w_down", [DFF, D])
    y_d = nc.dram_tensor("y", [NT, D], F32, kind="ExternalOutput").ap()
    xmid_d = nc.dram_tensor("xmid_scr", [E, D], F32, kind="Internal").ap()

    S = Sched()
    A = S.add
    out_ops = []

    with ExitStack() as st:
        ARENA = 212736
        arena = st.enter_context(nc.sbuf_tensor("arena", [128, ARENA // 2], BF16))
        psum = st.enter_context(nc.psum_tensor("psum", [128, 4096], F32))
        psum_bf = psum[:].bitcast(BF16)

        class Bump:
            def __init__(self):
                self.off = 0

            def alloc(self, shape, dt):
                esz = 4 if dt == F32 else 2
                n = int(np.prod(shape[1:]))
                nb = (n * esz + 31) // 32 * 32
                assert self.off + nb <= ARENA, ("SBUF overflow", self.off + nb)
                ap = arena[:, self.off // 2: self.off // 2 + n * esz // 2]
                self.off += nb
                if dt == F32:
                    ap = ap.bitcast(F32)
                if len(shape) == 3:
                    ap = ap.rearrange("p (a b) -> p a b", a=shape[1])
                elif len(shape) == 4:
                    ap = ap.rearrange("p (a b c) -> p a b c", a=shape[1], b=shape[2])
                return ap

        M = Bump()

        def bank(b, n=512, off=0):
            return psum[:, b * 512 + off: b * 512 + off + n]

        def bank_bf(b, n):
            return psum_bf[:, b * 1024: b * 1024 + n]

        uid = [0]

        def key(prefix):
            uid[0] += 1
            return (prefix, uid[0])

        ident = M.alloc([128, 128], BF16)
        identf = M.alloc([128, 128], F32)
        bones = M.alloc([128, 128], BF16)
        bonesf = M.alloc([128, 128], F32)
        gq = M.alloc([128, 6], F32)
        g1rep = M.alloc([128, D], F32)
        halo_m = M.alloc([128, 2], F32)
        A("pool", lambda e: e.memset(identf, 0.0), w=["identf"])
        A("pool", lambda e: e.affine_select(out=identf, in_=identf, pattern=[[-1, 128]], compare_op=ALU.not_equal,
                                            fill=1.0, base=0, channel_multiplier=1), r=["identf"], w=["identf"])
        A("pool", lambda e: e.tensor_copy(out=ident, in_=identf), r=["identf"], w=["ident"])
        A("pool", lambda e: e.memset(bonesf, 0.0), w=["bonesf"])
        A("pool", lambda e: e.memset(bonesf[0:64, 0:64], 1.0), r=["bonesf"], w=["bonesf"])
        A("pool", lambda e: e.memset(bonesf[64:128, 64:128], 1.0), r=["bonesf"], w=["bonesf"])
        A("pool", lambda e: e.tensor_copy(out=bones, in_=bonesf), r=["bonesf"], w=["bones"])
        A("sp", lambda e: e.dma_start(out=gq, in_=gq_d), w=["gq"], dma=True)
        A("sp", lambda e: e.dma_start(out=g1rep, in_=g1_d.partition_broadcast(128)), w=["g1rep"], dma=True)
        A("sp", lambda e: e.dma_start(out=halo_m, in_=halo_d), w=["halo_m"], dma=True)

        stage = [M.alloc([128, 512], F32) for _ in range(2)]
        qat_off = M.off
        QAT = M.alloc([128, 4, E], BF16)
        mark_persist = M.off

        cast_rr = [0]

        def load_cast(dst, src, rows, ncols, dkey, stage):
            for kc in range(rows // 128):
                for c0 in range(0, ncols, 512):
                    cn = min(512, ncols - c0)
                    i = cast_rr[0] % 2
                    cast_rr[0] += 1
                    sk = ("stage", i)
                    A("sp", lambda e, kc=kc, c0=c0, cn=cn, i=i: e.dma_start(
                        out=stage[i][:, 0:cn], in_=src[kc * 128:(kc + 1) * 128, c0:c0 + cn]), w=[sk], dma=True)
                    eng = ("pool", "act")[cast_rr[0] % 2]
                    if eng == "pool":
                        A("pool", lambda e, kc=kc, c0=c0, cn=cn, i=i: e.tensor_copy(out=dst[:, kc, c0:c0 + cn], in_=stage[i][:, 0:cn]),
                          r=[sk], w=[dkey])
                    else:
                        A("act", lambda e, kc=kc, c0=c0, cn=cn, i=i: e.activation(out=dst[:, kc, c0:c0 + cn], in_=stage[i][:, 0:cn], func=AF.Copy),
                          r=[sk], w=[dkey])

        def sweep(xsrc, ntok, wk, per_chunk):
            xb = [wk.alloc([128, D], F32) for _ in range(2)]
            junk = wk.alloc([128, D], BF16)
            hnb = [wk.alloc([128, D], BF16) for _ in range(2)]
            ssq = [wk.alloc([128, 1], F32) for _ in range(2)]
            rs = [wk.alloc([128, 1], F32) for _ in range(2)]
            hnT = [wk.alloc([128, KC, CH], BF16) for _ in range(2)]
            tno = 0
            pending = None
            for ci, k0 in enumerate(range(0, ntok, CH)):
                n = min(CH, ntok - k0)
                hk = ("hnT", ci % 2)
                for tt in range(n // 128):
                    i = tno % 2
                    r0 = k0 + tt * 128
                    A("sp", lambda e, i=i, r0=r0: e.dma_start(out=xb[i], in_=xsrc[r0:r0 + 128, :]), w=[("xb", i)], dma=True)
                    A("act", lambda e, i=i: e.activation(out=junk, in_=xb[i], func=AF.Square, accum_out=ssq[i]),
                      r=[("xb", i)], w=[("ssq", i), "junk"])
                    A("act", lambda e, i=i: e.activation(out=rs[i], in_=ssq[i], func=AF.Sqrt, scale=1.0 / D, bias=EPS),
                      r=[("ssq", i)], w=[("rs", i)])
                    A("dve", lambda e, i=i: e.reciprocal(out=rs[i], in_=rs[i]), r=[("rs", i)], w=[("rs", i)])
                    A("dve", lambda e, i=i: e.scalar_tensor_tensor(out=hnb[i], in0=xb[i], scalar=rs[i][:, 0:1], in1=g1rep,
                                                                   op0=ALU.mult, op1=ALU.mult),
                      r=[("xb", i), ("rs", i), "g1rep"], w=[("hnb", i)])
                    pb = tno % 2
                    for kc in range(KC):
                        A("pe", lambda e, i=i, kc=kc, pb=pb: e.transpose(out=bank_bf(pb, KC * 128)[:, kc * 128:(kc + 1) * 128],
                                                                        in_=hnb[i][:, kc * 128:(kc + 1) * 128], identity=ident),
                          r=[("hnb", i), "ident"], w=[("ps", pb)])
                    A("act", lambda e, ci=ci, tt=tt, pb=pb: e.activation(
                        out=hnT[ci % 2][:, :, tt * 128:(tt + 1) * 128],
                        in_=bank_bf(pb, KC * 128).rearrange("p (a b) -> p a b", a=KC), func=AF.Copy), r=[("ps", pb)], w=[hk])
                    tno += 1
                if pending is not None:
                    per_chunk(*pending)
                pending = (ci, k0, n, hnT[ci % 2], hk)
            if pending is not None:
                per_chunk(*pending)

        def proj_fm(wt, wcol, hT_, hk, c_lo, nq, pbank, wkey):
            for kc in range(KC):
                A("pe", lambda e, kc=kc: e.matmul(bank(pbank, nq), lhsT=wt[:, kc, wcol:wcol + 128], rhs=hT_[:, kc, c_lo:c_lo + nq],
                                                  start=(kc == 0), stop=(kc == KC - 1)), r=[hk, wkey], w=[("ps", pbank)])

        class NormWork:
            def __init__(self, wk):
                self.sqb = [wk.alloc([128, CH], BF16) for _ in range(2)]
                self.rs = [wk.alloc([128, CH], F32) for _ in range(2)]
                self.t1 = [wk.alloc([128, CH], F32) for _ in range(2)]
                self.t2 = [wk.alloc([128, CH], F32) for _ in range(2)]
                self.n = 0

        def qknorm(nw, pq, pqs, nq, gcol, gscol, cos, sin, roper, out_ap, okey):
            i = nw.n % 2
            nw.n += 1
            sqb, rs, t1, t2 = nw.sqb[i], nw.rs[i], nw.t1[i], nw.t2[i]
            A("act", lambda e: e.activation(out=sqb[:, 0:nq], in_=bank(pq, nq), func=AF.Square), r=[("ps", pq)], w=[("sqb", i)])
            A("pe", lambda e: e.matmul(bank(6, nq), lhsT=bones, rhs=sqb[:, 0:nq], start=True, stop=True),
              r=[("sqb", i), "bones"], w=[("ps", 6)])
            A("act", lambda e: e.activation(out=rs[:, 0:nq], in_=bank(6, nq), func=AF.Sqrt, scale=1.0, bias=64 * EPS),
              r=[("ps", 6)], w=[("nrs", i)])
            A("dve", lambda e: e.reciprocal(out=rs[:, 0:nq], in_=rs[:, 0:nq]), r=[("nrs", i)], w=[("nrs", i)])
            if pqs is None:
                A("dve", lambda e: e.scalar_tensor_tensor(out=out_ap, in0=bank(pq, nq), scalar=gq[:, gcol:gcol + 1], in1=rs[:, 0:nq],
                                                          op0=ALU.mult, op1=ALU.mult), r=[("ps", pq), ("nrs", i), "gq"], w=[okey])
                return
            A("dve", lambda e: e.scalar_tensor_tensor(out=t1[:, 0:nq], in0=bank(pq, nq), scalar=gq[:, gcol:gcol + 1], in1=rs[:, 0:nq],
                                                      op0=ALU.mult, op1=ALU.mult), r=[("ps", pq), ("nrs", i), "gq"], w=[("t1", i)])
            A("dve", lambda e: e.scalar_tensor_tensor(out=t2[:, 0:nq], in0=bank(pqs, nq), scalar=gq[:, gscol:gscol + 1], in1=rs[:, 0:nq],
                                                      op0=ALU.mult, op1=ALU.mult), r=[("ps", pqs), ("nrs", i), "gq"], w=[("t2", i)])
            A("pool", lambda e: e.tensor_tensor(out=t1[:, 0:nq], in0=t1[:, 0:nq], in1=cos, op=ALU.mult), r=[("t1", i)] + roper, w=[("t1", i)])
            A("pool", lambda e: e.tensor_tensor(out=t2[:, 0:nq], in0=t2[:, 0:nq], in1=sin, op=ALU.mult), r=[("t2", i)] + roper, w=[("t2", i)])
            A("pool", lambda e: e.tensor_tensor(out=out_ap, in0=t1[:, 0:nq], in1=t2[:, 0:nq], op=ALU.add),
              r=[("t1", i), ("t2", i)], w=[okey])

        class PostWork:
            def __init__(self, wk, ggrep, goff):
                self.rd = [wk.alloc([128, 8], F32) for _ in range(2)]
                self.outf = [wk.alloc([128, 512], F32) for _ in range(2)]
                self.junk = wk.alloc([128, 512], BF16)
                self.ssq = [wk.alloc([128, 1], F32) for _ in range(2)]
                self.rs = [wk.alloc([128, 1], F32) for _ in range(2)]
                self.mixb = [wk.alloc([128, 512], BF16) for _ in range(2)]
                self.ggrep = ggrep
                self.goff = goff
                self.n = 0

        def mixer_post(pw, obank0, okeys, sz, e0, mixT, mkey, tbank):
            i = pw.n % 2
            pw.n += 1
            rd, outf, mixb = pw.rd[i], pw.outf[i], pw.mixb[i]
            for g in range(2):
                ov = bank(obank0 + g, 260).rearrange("p (a b) -> p a b", a=4)
                A("dve", lambda e, g=g, ov=ov: e.reciprocal(out=rd[0:sz, g * 4:(g + 1) * 4], in_=ov[0:sz, :, 64]),
                  r=[okeys[g]], w=[("rd", i, g)])
                for j in range(4):
                    h = g * 4 + j
                    A("act", lambda e, h=h, j=j, ov=ov: e.activation(out=outf[0:sz, h * 64:(h + 1) * 64], in_=ov[0:sz, j, 0:64],
                                                                    func=AF.Copy, scale=rd[0:sz, h:h + 1]),
                      r=[okeys[g], ("rd", i, g)], w=[("outf", i)])
            A("act", lambda e: e.activation(out=pw.junk[0:sz, :], in_=outf[0:sz, :], func=AF.Square, accum_out=pw.ssq[i][0:sz, :]),
              r=[("outf", i)], w=[("pssq", i), "pjunk"])
            A("act", lambda e: e.activation(out=pw.rs[i][0:sz, :], in_=pw.ssq[i][0:sz, :], func=AF.Sqrt, scale=1.0 / 512, bias=EPS),
              r=[("pssq", i)], w=[("prs", i)])
            A("dve", lambda e: e.reciprocal(out=pw.rs[i][0:sz, :], in_=pw.rs[i][0:sz, :]), r=[("prs", i)], w=[("prs", i)])
            A("dve", lambda e: e.scalar_tensor_tensor(out=mixb[0:sz, :], in0=outf[0:sz, :], scalar=pw.rs[i][0:sz, 0:1],
                                                      in1=pw.ggrep[0:sz, pw.goff:pw.goff + 512], op0=ALU.mult, op1=ALU.mult),
              r=[("outf", i), ("prs", i), "ggrep"], w=[("mixb", i)])
            tv = bank_bf(tbank, 512).rearrange("p (a b) -> p a b", a=4)
            for c in range(4):
                A("pe", lambda e, c=c: e.transpose(out=tv[:, c, 0:sz], in_=mixb[0:sz, c * 128:(c + 1) * 128], identity=ident[0:sz, 0:sz]),
                  r=[("mixb", i), "ident"], w=[("ps", tbank)])
            A("dve", lambda e: e.tensor_copy(out=mixT[:, :, e0:e0 + sz], in_=tv[:, :, 0:sz]), r=[("ps", tbank)], w=[mkey])

        qtiles = [(0, 1)] + [(1 + 128 * i, 128) for i in range(NT // 128)] + [(NT + 1, 1)]

        KBT = M.alloc([128, 4, WIN], BF16)
        VB = M.alloc([128, NWT, 8, 65], BF16)
        QBT = M.alloc([128, 4, E], BF16)
        mark_w = M.off
        WinB = M.alloc([128, KC, 2560], BF16)
        load_cast(WinB, w_inB, D, 2560, "WinB", stage)
        vtmp = M.alloc([128, NWT, 8], F32)
        A("sp", lambda e: e.dma_start(out=vtmp, in_=vones_d), w=["vtmp"], dma=True)
        A("pool", lambda e: e.tensor_copy(out=VB[:, :, :, 64], in_=vtmp), r=["vtmp"], w=["VBones"])
        nw = NormWork(M)
        ropebW = [[M.alloc([128, CH], F32) for _ in range(2)] for _ in range(2)]
        vcnt = [0]

        def chunk_W(ci, k0, n, hT_, hk):
            for p in range(4):
                pb = 2 + (p % 2) * 2
                proj_fm(WinB, 1536 + p * 128, hT_, hk, 0, n, pb, "WinB")
                qknorm(nw, pb, None, n, 5, None, None, None, None, KBT[:, p, k0:k0 + n], ("KBT", ci))
            for tt in range(n // 128):
                t = (k0 // 128) + tt
                for kc in range(KC):
                    A("pe", lambda e, kc=kc, tt=tt: e.matmul(bank(7, 512), lhsT=hT_[:, kc, tt * 128:(tt + 1) * 128], rhs=WinB[:, kc, 2048:2560],
                                                             start=(kc == 0), stop=(kc == KC - 1)), r=[hk, "WinB"], w=[("ps", 7)])
                A("act", lambda e, t=t: e.activation(out=VB[:, t, :, 0:64], in_=bank(7, 512).rearrange("p (a b) -> p a b", a=8), func=AF.Copy),
                  r=[("ps", 7)], w=[("VB", t)])
            c_lo = max(k0, HALO - 1)
            c_hi = min(k0 + n, HALO - 1 + E)
            if c_lo < c_hi:
                nq = c_hi - c_lo
                e_lo = c_lo - (HALO - 1)
                off = c_lo - k0
                ri = ci % 2
                rk = ("ropeq", ri)
                rb = ropebW[ri]
                A("sp", lambda e: e.dma_start(out=rb[0][:, 0:nq], in_=ropeq_d[0, :, e_lo:e_lo + nq]), w=[rk], dma=True)
                A("sp", lambda e: e.dma_start(out=rb[1][:, 0:nq], in_=ropeq_d[1, :, e_lo:e_lo + nq]), w=[rk], dma=True)
                for p in range(4):
                    pb = 2 + (p % 2) * 2
                    proj_fm(WinB, 1024 + p * 128, hT_, hk, off, nq, pb, "WinB")
                    qknorm(nw, pb, None, nq, 4, None, None, None, None, QBT[:, p, e_lo:e_lo + nq], "QBT")
                for p in range(4):
                    pb = 2 + (p % 2) * 2
                    proj_fm(WinB, p * 128, hT_, hk, off, nq, pb, "WinB")
                    proj_fm(WinB, 512 + p * 128, hT_, hk, off, nq, pb + 1, "WinB")
                    qknorm(nw, pb, pb + 1, nq, 0, 1, rb[0][:, 0:nq], rb[1][:, 0:nq], [rk],
                           QAT[:, p, e_lo:e_lo + nq], "QAT")

        sweep(xwin, WIN, M, chunk_W)
        S.barrier()

        M.off = mark_w
        mixTB = M.alloc([128, 4, E], BF16)
        mixTB_end = M.off
        maskB = M.alloc([128, 8, MW], BF16)
        A("sp", lambda e: e.dma_start(out=maskB, in_=maskB_d), w=["maskB"], dma=True)
        ggrep = M.alloc([128, 1024], F32)
        A("sp", lambda e: e.dma_start(out=ggrep, in_=gg_d.partition_broadcast(128)), w=["ggrep"], dma=True)
        qz = [M.alloc([128, 8, 128], BF16) for _ in range(2)]
        Eb = [M.alloc([128, 8, 128], BF16) for _ in range(2)]
        PT = [M.alloc([128, 8, 128], BF16) for _ in range(2)]
        for i in range(2):
            A("pool", lambda e, i=i: e.memset(qz[i], 0.0), w=[("qz", i)])
        pwB = PostWork(M, ggrep, 512)
        anyKBT = [("KBT", ci) for ci in range((WIN + CH - 1) // CH)]
        def tile_B(qi, e0, sz):
            zi = qi % 2
            kq0 = e0 + HALO - 1
            kt_lo = max(0, (kq0 - 1024) // 128)
            kt_hi = min(NWT - 1, (kq0 + sz - 1 + 1024) // 128)
            qv = qz[zi].rearrange("p (a b) c -> p a b c", b=2)
            A("pool", lambda e, qv=qv, e0=e0, sz=sz: e.tensor_copy(out=qv[0:64, :, 0, 0:sz], in_=QBT[0:64, :, e0:e0 + sz]),
              r=["QBT"], w=[("qz", zi)])
            A("pool", lambda e, qv=qv, e0=e0, sz=sz: e.tensor_copy(out=qv[64:128, :, 1, 0:sz], in_=QBT[64:128, :, e0:e0 + sz]),
              r=["QBT"], w=[("qz", zi)])
            ob = 4
            okeys = [("ps", 4), ("ps", 5)]
            kts = list(range(kt_lo, kt_hi + 1))

            def qk(n_, kt):
                sb_ = (n_ % 2) * 2
                sv = psum[:, sb_ * 512: sb_ * 512 + 1024].rearrange("p (a b) -> p a b", a=8)
                for h in range(8):
                    A("pe", lambda e, h=h, kt=kt, sv=sv: e.matmul(sv[:, h, 0:sz], lhsT=KBT[:, h // 2, kt * 128:(kt + 1) * 128],
                                                                  rhs=qz[zi][:, h, 0:sz], start=True, stop=True),
                      r=[("qz", zi)] + anyKBT[(kt * 128) // CH:(kt * 128) // CH + 1], w=[("S", n_ % 2)])

            def ex(n_, kt):
                sb_ = (n_ % 2) * 2
                sv = psum[:, sb_ * 512: sb_ * 512 + 1024].rearrange("p (a b) -> p a b", a=8)
                bi = n_ % 2
                A("act", lambda e, sv=sv, bi=bi: e.activation(out=Eb[bi][:, :, 0:sz], in_=sv[:, :, 0:sz], func=AF.Exp, scale=8.0),
                  r=[("S", n_ % 2)], w=[("Eb", bi)])
                c0 = kq0 - 128 * kt + C0
                A("dve", lambda e, bi=bi, c0=c0: e.tensor_tensor(out=PT[bi][:, :, 0:sz], in0=Eb[bi][:, :, 0:sz],
                                                               in1=maskB[:, :, c0:c0 + sz], op=ALU.mult),
                  r=[("Eb", bi), "maskB"], w=[("PT", bi)])

            def pv(n_, kt):
                bi = n_ % 2
                for h in range(8):
                    g, j = h // 4, h % 4
                    A("pe", lambda e, h=h, g=g, j=j, kt=kt, bi=bi, n_=n_: e.matmul(
                        bank(ob + g, 65, j * 65)[0:sz, :], lhsT=PT[bi][:, h, 0:sz], rhs=VB[:, kt, h, :],
                        start=(n_ == 0 and j == 0), stop=(n_ == len(kts) - 1 and j == 3)),
                      r=[("PT", bi), ("VB", kt), "VBones"], w=[okeys[g]])

            for n_, kt in enumerate(kts):
                qk(n_, kt)
                if n_ >= 1:
                    pv(n_ - 1, kts[n_ - 1])
                ex(n_, kt)
            pv(len(kts) - 1, kts[-1])
            mixer_post(pwB, ob, okeys, sz, e0, mixTB, "mixTB", 6)

        for qi, (e0, sz) in enumerate(qtiles):
            tile_B(qi, e0, sz)
        S.barrier()

        M.off = mark_persist
        hT = M.alloc([128, KC, E], BF16)
        KAT = M.alloc([128, SEQ], BF16)
        VA = M.alloc([128, NKT, 2, 65], BF16)
        mixTA = M.alloc([128, 4, E], BF16)
        M.off = max(M.off, mixTB_end)
        mark_a = M.off
        WinA = M.alloc([128, KC, 384], BF16)
        load_cast(WinA, w_inA, D, 384, "WinA", stage)
        A("pool", lambda e: e.memset(VA[:, :, :, 64], 1.0), w=["VAones"])
        nw = NormWork(M)
        ropebA = [[M.alloc([128, CH], F32) for _ in range(2)] for _ in range(2)]

        def chunk_A(ci, k0, n, hT_, hk):
            ri = ci % 2
            rk = ("ropek", ri)
            rb = ropebA[ri]
            A("sp", lambda e: e.dma_start(out=rb[0][:, 0:n], in_=ropek_d[0, :, k0:k0 + n]), w=[rk], dma=True)
            A("sp", lambda e: e.dma_start(out=rb[1][:, 0:n], in_=ropek_d[1, :, k0:k0 + n]), w=[rk], dma=True)
            pb = 2 + (ci % 2) * 2
            proj_fm(WinA, 0, hT_, hk, 0, n, pb, "WinA")
            proj_fm(WinA, 128, hT_, hk, 0, n, pb + 1, "WinA")
            qknorm(nw, pb, pb + 1, n, 2, 3, rb[0][:, 0:n], rb[1][:, 0:n], [rk], KAT[:, k0:k0 + n], ("KAT", ci))
            for tt in range(n // 128):
                t = (k0 // 128) + tt
                for kc in range(KC):
                    A("pe", lambda e, kc=kc, tt=tt: e.matmul(bank(7, 128), lhsT=hT_[:, kc, tt * 128:(tt + 1) * 128], rhs=WinA[:, kc, 256:384],
                                                             start=(kc == 0), stop=(kc == KC - 1)), r=[hk, "WinA"], w=[("ps", 7)])
                A("act", lambda e, t=t: e.activation(out=VA[:, t, :, 0:64], in_=bank(7, 128).rearrange("p (a b) -> p a b", a=2), func=AF.Copy),
                  r=[("ps", 7)], w=[("VA", t)])

        sweep(xfull, SEQ, M, chunk_A)
        S.barrier()

        M.off = mark_a
        Wout = M.alloc([128, 8, D], BF16)
        load_cast(Wout, w_out_d, 1024, D, "Wout", stage)
        ggrepA = M.alloc([128, 1024], F32)
        A("sp", lambda e: e.dma_start(out=ggrepA, in_=gg_d.partition_broadcast(128)), w=["ggrep"], dma=True)
        g2rep = M.alloc([128, D], F32)
        A("sp", lambda e: e.dma_start(out=g2rep, in_=g2_d.partition_broadcast(128)), w=["g2rep"], dma=True)
        qzA = [M.alloc([128, 2, 4, 128], BF16) for _ in range(2)]
        PTA = [M.alloc([128, 2, 4, 128], BF16) for _ in range(2)]
        for i in range(2):
            A("pool", lambda e, i=i: e.memset(qzA[i], 0.0), w=[("qzA", i)])
        pwA = PostWork(M, ggrepA, 0)
        xr = [M.alloc([128, D], F32) for _ in range(2)]
        xm = [M.alloc([128, D], F32) for _ in range(2)]
        hb = [M.alloc([128, D], BF16) for _ in range(2)]
        junk2 = M.alloc([128, D], BF16)
        ssq2 = [M.alloc([128, 1], F32) for _ in range(2)]
        rs2 = [M.alloc([128, 1], F32) for _ in range(2)]
        anyKAT = [("KAT", ci) for ci in range(SEQ // CH)]
        def tile_A(qi, e0, sz):
            zi = qi % 2
            kq0 = e0 + HALO - 1
            A("pool", lambda e, e0=e0, sz=sz, zi=zi: e.tensor_copy(out=qzA[zi][0:64, 0, :, 0:sz], in_=QAT[0:64, :, e0:e0 + sz]),
              r=["QAT"], w=[("qzA", zi)])
            A("pool", lambda e, e0=e0, sz=sz, zi=zi: e.tensor_copy(out=qzA[zi][64:128, 1, :, 0:sz], in_=QAT[64:128, :, e0:e0 + sz]),
              r=["QAT"], w=[("qzA", zi)])
            A("sp", lambda e, zi=zi, kq0=kq0, sz=sz: e.dma_start(out=xr[zi][0:sz, :], in_=xwin[kq0:kq0 + sz, :]), w=[("xr", zi)], dma=True)
            ob = 4
            okeys = [("ps", 4), ("ps", 5)]

            def qkA(kt):
                sb_ = (kt % 2) * 2
                sv = psum[:, sb_ * 512: sb_ * 512 + 1024].rearrange("p (g a b) -> p g a b", g=2, a=4)
                for g in range(2):
                    A("pe", lambda e, g=g, kt=kt, sv=sv: e.matmul(sv[:, g, :, 0:sz], lhsT=KAT[:, kt * 128:(kt + 1) * 128],
                                                                  rhs=qzA[zi][:, g, :, 0:sz], start=True, stop=True),
                      r=[("qzA", zi), anyKAT[(kt * 128) // CH]], w=[("S", kt % 2)])

            def exA(kt):
                sb_ = (kt % 2) * 2
                sv = psum[:, sb_ * 512: sb_ * 512 + 1024].rearrange("p (g a b) -> p g a b", g=2, a=4)
                bi = kt % 2
                A("act", lambda e, sv=sv, bi=bi: e.activation(out=PTA[bi][:, :, :, 0:sz], in_=sv[:, :, :, 0:sz], func=AF.Exp, scale=8.0),
                  r=[("S", kt % 2)], w=[("PTA", bi)])

            def pvA(kt):
                bi = kt % 2
                for g in range(2):
                    for j in range(4):
                        A("pe", lambda e, g=g, j=j, kt=kt, bi=bi: e.matmul(
                            bank(ob + g, 65, j * 65)[0:sz, :], lhsT=PTA[bi][:, g, j, 0:sz], rhs=VA[:, kt, g, :],
                            start=(kt == 0 and j == 0), stop=(kt == NKT - 1 and j == 3)),
                          r=[("PTA", bi), ("VA", kt), "VAones"], w=[okeys[g]])

            for kt in range(NKT):
                qkA(kt)
                if kt >= 1:
                    pvA(kt - 1)
                exA(kt)
            pvA(NKT - 1)
            mixer_post(pwA, ob, okeys, sz, e0, mixTA, "mixTA", 6)
            for nb_ in range(NB):
                for kc in range(8):
                    src = mixTA[:, kc] if kc < 4 else mixTB[:, kc - 4]
                    A("pe", lambda e, nb_=nb_, kc=kc, src=src: e.matmul(bank(nb_, 512)[0:sz, :], lhsT=src[:, e0:e0 + sz],
                                                                        rhs=Wout[:, kc, nb_ * 512:(nb_ + 1) * 512],
                                                                        start=(kc == 0), stop=(kc == 7)),
                      r=["mixTA", "mixTB", "Wout"], w=[("S", nb_ // 2)] if NB > 1 else [("S", 0)])
            ykeys = [("S", nb_ // 2) for nb_ in range(NB)] if NB > 1 else [("S", 0)]
            for nb_ in range(NB):
                A("dve", lambda e, nb_=nb_, zi=zi: e.tensor_tensor(out=xm[zi][0:sz, nb_ * 512:(nb_ + 1) * 512], in0=bank(nb_, 512)[0:sz, :],
                                                                   in1=xr[zi][0:sz, nb_ * 512:(nb_ + 1) * 512], op=ALU.add),
                  r=[ykeys[nb_], ("xr", zi)], w=[("xm", zi)])
            A("sp", lambda e, zi=zi, e0=e0, sz=sz: e.dma_start(out=xmid_d[e0:e0 + sz, :], in_=xm[zi][0:sz, :]),
              r=[("xm", zi)], w=[("xmid_d", e0)], dma=True)
            A("act", lambda e, zi=zi: e.activation(out=junk2[0:sz, :], in_=xm[zi][0:sz, :], func=AF.Square, accum_out=ssq2[zi][0:sz, :]),
              r=[("xm", zi)], w=[("ssq2", zi), "junk2"])
            A("act", lambda e, zi=zi: e.activation(out=rs2[zi][0:sz, :], in_=ssq2[zi][0:sz, :], func=AF.Sqrt, scale=1.0 / D, bias=EPS),
              r=[("ssq2", zi)], w=[("rs2", zi)])
            A("dve", lambda e, zi=zi: e.reciprocal(out=rs2[zi][0:sz, :], in_=rs2[zi][0:sz, :]), r=[("rs2", zi)], w=[("rs2", zi)])
            A("dve", lambda e, zi=zi: e.scalar_tensor_tensor(out=hb[zi][0:sz, :], in0=xm[zi][0:sz, :], scalar=rs2[zi][0:sz, 0:1],
                                                             in1=g2rep[0:sz, :], op0=ALU.mult, op1=ALU.mult),
              r=[("xm", zi), ("rs2", zi), "g2rep"], w=[("hb", zi)])
            tv = bank_bf(7, KC * 128).rearrange("p (a b) -> p a b", a=KC)
            for kc in range(KC):
                A("pe", lambda e, kc=kc, zi=zi: e.transpose(out=tv[:, kc, 0:sz], in_=hb[zi][0:sz, kc * 128:(kc + 1) * 128],
                                                            identity=ident[0:sz, 0:sz]), r=[("hb", zi), "ident"], w=[("ps", 7)])
            A("dve", lambda e, e0=e0, sz=sz: e.tensor_copy(out=hT[:, :, e0:e0 + sz], in_=tv[:, :, 0:sz]), r=[("ps", 7)], w=["hT"])

        for qi, (e0, sz) in enumerate(qtiles):
            tile_A(qi, e0, sz)
        A("act", lambda e: e.activation(out=hT[:, :, 0:1], in_=hT[:, :, 0:1], func=AF.Copy, scale=halo_m[:, 0:1]),
          r=["hT", "halo_m"], w=["hT"])
        A("act", lambda e: e.activation(out=hT[:, :, E - 1:E], in_=hT[:, :, E - 1:E], func=AF.Copy, scale=halo_m[:, 1:2]),
          r=["hT", "halo_m"], w=["hT"])
        S.barrier()

        M.off = mark_persist + (KC * E * 2 + 31) // 32 * 32
        actT = M.alloc([128, NJ, HN], BF16)
        Wd = M.alloc([128, NJ, D], BF16)
        convw = M.alloc([128, 2 * NJ, 3], F32)
        convb = M.alloc([128, 2 * NJ], F32)
        A("sp", lambda e: e.dma_start(out=convw, in_=convw_d), w=["convw"], dma=True)
        A("sp", lambda e: e.dma_start(out=convb, in_=convb_d), w=["convb"], dma=True)
        load_cast(Wd, w_down_d, DFF, D, "Wd", stage)
        wst = [M.alloc([128, KC, 256], F32) for _ in range(2)]
        wub = [M.alloc([128, KC, 256], BF16) for _ in range(2)]
        cg = [M.alloc([128, HN], F32) for _ in range(2)]
        cv = [M.alloc([128, HN], F32) for _ in range(2)]
        tb = [M.alloc([128, HN], F32) for _ in range(2)]
        bconst = M.alloc([128, 1], F32)
        A("pool", lambda e: e.memset(bconst, 1.5957691216057308), w=["bconst"])
        if qat_off + 4 * D * 4 <= mark_persist:
            _save = M.off
            M.off = qat_off
            xmF = [M.alloc([128, D], F32) for _ in range(2)]
            yt = [M.alloc([128, D], F32) for _ in range(2)]
            M.off = _save
        else:
            xmF = [M.alloc([128, D], F32) for _ in range(2)]
            yt = [M.alloc([128, D], F32) for _ in range(2)]
        W = HN + 2
        pieces = [(c, min(512, W - c)) for c in range(0, W, 512)]
        wup_v = w_up_d.rearrange("(kc p) c -> p kc c", p=128)

        def ffn_j(hf, j, it):
            eb = HN * hf
            wi = it % 2
            A("sp", lambda e: e.dma_start(out=wst[wi][:, :, 0:128], in_=wup_v[:, :, j * 128:(j + 1) * 128]),
              w=[("wst", wi)], dma=True)
            A("sp", lambda e: e.dma_start(out=wst[wi][:, :, 128:256], in_=wup_v[:, :, DFF + j * 128:DFF + (j + 1) * 128]),
              w=[("wst", wi)], dma=True)
            A("act", lambda e: e.activation(out=wub[wi], in_=wst[wi], func=AF.Copy), r=[("wst", wi)], w=[("wub", wi)])
            for half, b0 in ((0, 0), (1, 3)):
                for pi, (c, cn) in enumerate(pieces):
                    for kc in range(KC):
                        A("pe", lambda e, kc=kc, c=c, cn=cn, b0=b0, pi=pi, half=half: e.matmul(
                            bank(b0 + pi, cn), lhsT=wub[wi][:, kc, half * 128:(half + 1) * 128], rhs=hT[:, kc, eb + c:eb + c + cn],
                            start=(kc == 0), stop=(kc == KC - 1)), r=[("wub", wi), "hT"], w=[("G", half)])
            ci_ = j % 2
            for half, b0, cbuf in ((0, 0, cg[ci_]), (1, 3, cv[ci_])):
                jj = j + half * NJ
                P0 = psum[:, b0 * 512: b0 * 512 + W]
                ck = ("c", half, ci_)
                A("act", lambda e, P0=P0, cbuf=cbuf, jj=jj: e.activation(out=cbuf, in_=P0[:, 1:HN + 1], func=AF.Identity,
                                                                        scale=convw[:, jj, 1:2], bias=convb[:, jj:jj + 1]),
                  r=[("G", half), "convw", "convb"], w=[ck])
                A("dve", lambda e, P0=P0, cbuf=cbuf, jj=jj: e.scalar_tensor_tensor(out=cbuf, in0=P0[:, 0:HN], scalar=convw[:, jj, 0:1],
                                                                                  in1=cbuf, op0=ALU.mult, op1=ALU.add),
                  r=[("G", half), "convw", ck], w=[ck])
                A("dve", lambda e, P0=P0, cbuf=cbuf, jj=jj: e.scalar_tensor_tensor(out=cbuf, in0=P0[:, 2:HN + 2], scalar=convw[:, jj, 2:3],
                                                                                  in1=cbuf, op0=ALU.mult, op1=ALU.add),
                  r=[("G", half), "convw", ck], w=[ck])
            cgb, cvb = cg[ci_], cv[ci_]
            gk, vk = ("c", 0, ci_), ("c", 1, ci_)
            tbi = tb[ci_]
            tk = ("tb", ci_)
            A("act", lambda e: e.activation(out=tbi, in_=cgb, func=AF.Square, scale=math.sqrt(0.0713548162726)), r=[gk], w=[tk])
            A("dve", lambda e: e.scalar_tensor_tensor(out=tbi, in0=tbi, scalar=bconst[:, 0:1], in1=cgb, op0=ALU.add, op1=ALU.mult),
              r=[tk, gk, "bconst"], w=[tk])
            A("act", lambda e: e.activation(out=tbi, in_=tbi, func=AF.Sigmoid), r=[tk], w=[tk])
            A("pool", lambda e: e.tensor_tensor(out=cvb, in0=cgb, in1=cvb, op=ALU.mult), r=[gk, vk], w=[vk])
            A("dve", lambda e: e.tensor_tensor(out=actT[:, j, :], in0=cvb, in1=tbi, op=ALU.mult), r=[vk, tk], w=["actT"])

        def down_tile(hf, ti):
            eb = HN * hf
            zi = ti % 2
            e0 = 1 + eb + 128 * ti
            A("sp", lambda e: e.dma_start(out=xmF[zi], in_=xmid_d[e0:e0 + 128, :]),
              r=[("xmid_d", e0)], w=[("xmF", zi)], dma=True)
            for nb_ in range(NB):
                for j in range(NJ):
                    A("pe", lambda e, nb_=nb_, j=j: e.matmul(bank(6 + nb_ % 2, 512), lhsT=actT[:, j, ti * 128:(ti + 1) * 128],
                                                             rhs=Wd[:, j, nb_ * 512:(nb_ + 1) * 512],
                                                             start=(j == 0), stop=(j == NJ - 1)),
                      r=["actT", "Wd"], w=[("ps", 6 + nb_ % 2)])
                A("dve", lambda e, nb_=nb_: e.tensor_tensor(out=yt[zi][:, nb_ * 512:(nb_ + 1) * 512], in0=bank(6 + nb_ % 2, 512),
                                                            in1=xmF[zi][:, nb_ * 512:(nb_ + 1) * 512], op=ALU.add),
                  r=[("ps", 6 + nb_ % 2), ("xmF", zi)], w=[("yt", zi)])
            out_ops.append(A("sp", lambda e: e.dma_start(out=y_d[e0 - 1:e0 - 1 + 128, :], in_=yt[zi]),
                             r=[("yt", zi)], dma=True))

        it = 0
        for hf in range(2):
            for j in range(NJ):
                ffn_j(hf, j, it)
                it += 1
            for ti in range(HN // 128):
                down_tile(hf, ti)
        S.emit(nc, st, final_wait_ops=out_ops)
    return nc


def _swap_idx():
    idx = np.arange(64)
    sw = idx.copy()
    sw[0:16], sw[16:32], sw[32:48], sw[48:64] = idx[16:32], idx[0:16], idx[48:64], idx[32:48]
    return sw


def _rope_tables(pos):
    pos = np.asarray(pos, dtype=np.float64)
    valid = pos >= 0
    p = np.where(valid, pos, 0)
    row = np.floor(p / 64).astype(np.float32)
    col = (p % 64).astype(np.float32)
    inv = (np.float32(10000.0) ** (-np.arange(0, 32, 2, dtype=np.float32) / np.float32(32))).astype(np.float32)
    ar = (row[:, None] * inv[None, :]).astype(np.float32)
    ac = (col[:, None] * inv[None, :]).astype(np.float32)
    cos = np.concatenate([np.cos(ar), np.cos(ar), np.cos(ac), np.cos(ac)], axis=1).T
    sin = np.concatenate([-np.sin(ar), np.sin(ar), -np.sin(ac), np.sin(ac)], axis=1).T
    cos = np.concatenate([cos, cos], axis=0)
    sin = np.concatenate([sin, sin], axis=0)
    return np.ascontiguousarray(np.stack([cos, sin]).astype(np.float32))


def _maskB():
    ik = np.arange(128)[:, None]
    c = np.arange(MW)[None, :]
    dl = (c - C0 - ik).astype(np.int64)
    ad = np.abs(dl)
    m = (ad <= 64).astype(np.float64) + ((ad <= 256) & (dl % 4 == 0)) + ((ad <= 1024) & (dl % 16 == 0))
    out = np.zeros((128, 8, MW), np.float32)
    for h in range(8):
        slope = 2.0 ** (-8.0 * (h + 1) / 8)
        out[:, h, :] = m * np.exp(-slope * ad)
    return out.astype(ml_dtypes.bfloat16)


_CACHE = {}


def _run(x, norm1_g, w_in, qa_norm_g, ka_norm_g, qb_norm_g, kb_norm_g, outa_norm_g, outb_norm_g,
         w_out, norm2_g, w_up, conv_w, conv_b, w_down, runner=None):
    f = lambda a: np.ascontiguousarray(np.asarray(a, dtype=np.float32))
    x, w_in, w_out, w_up, w_down = f(x), f(w_in), f(w_out), f(w_up), f(w_down)
    B, SEQ, D = x.shape
    DFF = w_down.shape[0]
    NT = SEQ // 4
    NJ = DFF // 128
    WIN = NT + 2 * HALO
    NWT = WIN // 128
    E = NT + 2
    sw = _swap_idx()
    qa = w_in[:, 0:512].reshape(D, 8, 64)
    pair = np.stack([np.concatenate([qa[:, p], qa[:, 4 + p]], axis=1) for p in range(4)], axis=1)
    pair_s = np.stack([np.concatenate([qa[:, p][:, sw], qa[:, 4 + p][:, sw]], axis=1) for p in range(4)], axis=1)
    ka = w_in[:, 512:640].reshape(D, 2, 64)
    ka_s = ka[:, :, sw]
    va = w_in[:, 640:768]
    qb, kb, vb = w_in[:, 768:1280], w_in[:, 1280:1792], w_in[:, 1792:2304]
    w_inB = np.ascontiguousarray(np.concatenate([pair.reshape(D, 512), pair_s.reshape(D, 512), qb, kb, vb], axis=1))
    w_inA = np.ascontiguousarray(np.concatenate([ka.reshape(D, 128), ka_s.reshape(D, 128), va], axis=1))
    d2 = lambda g: np.concatenate([f(g), f(g)])
    gq = np.ascontiguousarray(np.stack([d2(qa_norm_g), d2(f(qa_norm_g)[sw]), d2(ka_norm_g), d2(f(ka_norm_g)[sw]),
                                        d2(qb_norm_g), d2(kb_norm_g)], axis=1))
    gg = np.ascontiguousarray(np.concatenate([f(outa_norm_g), f(outb_norm_g)]))
    convw = np.ascontiguousarray(f(conv_w).T.reshape(2 * NJ, 128, 3).transpose(1, 0, 2))
    convb = np.ascontiguousarray(f(conv_b).reshape(2 * NJ, 128).T)
    ropek = _rope_tables(np.arange(SEQ))
    maskB = _maskB()
    shared = dict(w_inB=w_inB, w_inA=w_inA, gq=gq, g1=f(norm1_g), ropek=ropek, maskB=maskB, w_out=w_out, gg=gg,
                  g2=f(norm2_g), w_up=w_up, convw=convw, convb=convb, w_down=w_down)
    in_maps = []
    for c in range(8):
        b, qtr = c // 4, c % 4
        T0 = qtr * NT
        xpad = np.zeros((SEQ + 2 * HALO, D), np.float32)
        xpad[HALO:HALO + SEQ] = x[b]
        xwin = np.ascontiguousarray(xpad[T0:T0 + WIN])
        tpos = T0 - HALO + np.arange(WIN)
        valid = ((tpos >= 0) & (tpos < SEQ)).astype(np.float32)
        vones = np.ascontiguousarray(np.repeat(valid.reshape(NWT, 128).T[:, :, None], 8, axis=2))
        epos = T0 - 1 + np.arange(E)
        ropeq = _rope_tables(np.where((epos >= 0) & (epos < SEQ), epos, -1))
        halo_m = np.zeros((128, 2), np.float32)
        halo_m[:, 0] = 1.0 if T0 - 1 >= 0 else 0.0
        halo_m[:, 1] = 1.0 if T0 + NT < SEQ else 0.0
        m = dict(shared)
        m.update(xfull=np.ascontiguousarray(x[b]), xwin=xwin, vones=vones, ropeq=ropeq, halo_m=halo_m)
        in_maps.append(m)
    ck = (D, SEQ, NT, DFF)
    if ck not in _CACHE:
        _CACHE[ck] = build(D, SEQ, NT, DFF)
    nc = _CACHE[ck]
    if runner is not None:
        outs = runner(nc, in_maps)
    else:
        res = run_bass_kernel_spmd(nc, in_maps, core_ids=list(range(8)))
        outs = [r["y"] for r in res.results]
    y = np.zeros((B, SEQ, D), np.float32)
    for c in range(8):
        b, qtr = c // 4, c % 4
        y[b, qtr * NT:(qtr + 1) * NT] = np.asarray(outs[c]).reshape(NT, D)
    return y


def kernel(**inputs):
    return _run(**inputs)
```

```python
import math
from collections import defaultdict
from contextlib import ExitStack

import numpy as np
import ml_dtypes

import concourse.bass as bass
import concourse.mybir as mybir
from concourse.bass_utils import run_bass_kernel_spmd

F32 = mybir.dt.float32
BF16 = mybir.dt.bfloat16
AF = mybir.ActivationFunctionType
ALU = mybir.AluOpType

HALO = 1152
import os as _os
DEPTH = int(_os.environ.get('MK_DEPTH', '3'))
SPLIT_CAST = _os.environ.get('MK_SPLIT', '1') == '1'
USE_LN = _os.environ.get('MK_LN', '0') == '1'
DEP_REDUCE = _os.environ.get('MK_DR', '1') == '1'
C0 = 1151
MW = 2430
EPS = 1e-6


class Op:
    __slots__ = ("eng", "fn", "reads", "writes", "dma", "deps", "sig", "needs_sig", "prev_dma", "idx")

    def __init__(self, eng, fn, reads, writes, dma):
        self.eng, self.fn, self.reads, self.writes, self.dma = eng, fn, reads, writes, dma
        self.deps = []
        self.sig = None
        self.needs_sig = False
        self.prev_dma = None


class Sched:
    def __init__(self, n_dma_sems=8):
        self.ops = []
        self.last_writer = {}
        self.readers = defaultdict(list)
        self.n_dma_sems = n_dma_sems
        self.last_on_eng = {}
        self.pending_barrier = {}
        self.outstanding_dma = []

    def barrier(self):
        deps = list(self.last_on_eng.values()) + list(self.outstanding_dma)
        self.outstanding_dma = []
        for e in ("pe", "act", "dve", "pool", "sp"):
            self.pending_barrier[e] = list(self.pending_barrier.get(e, [])) + deps

    def add(self, eng, fn, r=(), w=(), dma=False):
        op = Op(eng, fn, tuple(r), tuple(w), dma)
        op.idx = len(self.ops)
        deps = set()
        for b in op.reads:
            lw = self.last_writer.get(b)
            if lw is not None:
                deps.add(lw)
        for b in op.writes:
            lw = self.last_writer.get(b)
            if lw is not None:
                deps.add(lw)
            for rd in self.readers.get(b, ()):
                deps.add(rd)
        fdeps = []
        for d in deps:
            if d.eng == op.eng and not d.dma and not op.dma:
                if op.eng == "pe":
                    continue
                if op.eng != "pool" and not any(b in d.writes for b in op.reads):
                    continue
            fdeps.append(d)
        pb = self.pending_barrier.pop(eng, None)
        if pb:
            for d in pb:
                if d is not op and (d.dma or d.eng != eng):
                    fdeps.append(d)
        best = {}
        keep = []
        for d in fdeps:
            if d.dma or not DEP_REDUCE:
                keep.append(d)
            else:
                b = best.get(d.eng)
                if b is None or d.idx > b.idx:
                    best[d.eng] = d
        fdeps = keep + list(best.values())
        op.deps = fdeps
        for d in fdeps:
            d.needs_sig = True
        for b in op.reads:
            self.readers[b].append(op)
        for b in op.writes:
            self.last_writer[b] = op
            self.readers[b] = []
        self.ops.append(op)
        if dma:
            self.outstanding_dma.append(op)
        else:
            self.last_on_eng[eng] = op
        return op

    def emit(self, nc, stack, final_wait_ops=()):
        sems = {e: stack.enter_context(nc.semaphore("s_" + e)) for e in ("pe", "act", "dve", "pool")}
        dma_sems = [stack.enter_context(nc.semaphore("d_sp%d" % i)) for i in range(self.n_dma_sems)]
        cnt = defaultdict(int)
        rr = 0
        dcnt = defaultdict(int)
        dprev = {}
        for op in self.ops:
            if op.dma:
                i = rr % self.n_dma_sems
                rr += 1
                dcnt[i] += 1
                op.sig = (dma_sems[i], 16 * dcnt[i], 16)
                op.prev_dma = dprev.get(i)
                dprev[i] = op
            elif op.needs_sig:
                cnt[op.eng] += 1
                op.sig = (sems[op.eng], cnt[op.eng], 1)
        per_eng = defaultdict(list)
        for op in self.ops:
            per_eng[op.eng].append(op)
        block = stack.enter_context(nc.Block())

        def run(e, name):
            waited = {}

            def wait(sig):
                sem, val, _ = sig
                k = id(sem)
                if waited.get(k, 0) < val:
                    e.wait_ge(sem, val)
                    waited[k] = val

            for op in per_eng[name]:
                for d in op.deps:
                    wait(d.sig)
                if op.dma and op.prev_dma is not None:
                    wait(op.prev_dma.sig)
                ins = op.fn(e)
                if op.sig is not None:
                    ins.then_inc(op.sig[0], op.sig[2])
            if name == "sp":
                for op in final_wait_ops:
                    wait(op.sig)

        @block.tensor
        def _(e):
            run(e, "pe")

        @block.scalar
        def _(e):
            run(e, "act")

        @block.vector
        def _(e):
            run(e, "dve")

        @block.gpsimd
        def _(e):
            run(e, "pool")

        @block.sync
        def _(e):
            run(e, "sp")


def build(D, SEQ, NT, DFF):
    KC = D // 128
    NB = D // 512
    WIN = NT + 2 * HALO
    NWT = WIN // 128
    E = NT + 2
    NJ = DFF // 128
    NKT = SEQ // 128
    HN = NT // 2
    CH = 256

    nc = bass.Bass("TRN2", target_bir_lowering=False)

    def din(name, shape, dt=F32):
        return nc.dram_tensor(name, list(shape), dt, kind="ExternalInput").ap()

    xfull = din("xfull", [SEQ, D])
    xwin = din("xwin", [WIN, D])
    w_inB = din("w_inB", [D, 2560])
    w_inA = din("w_inA", [D, 384])
    gq_d = din("gq", [128, 6])
    g1_d = din("g1", [D])
    ropeq_d = din("ropeq", [2, 128, E])
    ropek_d = din("ropek", [2, 128, SEQ])
    maskB_d = din("maskB", [128, 8, MW], BF16)
    vones_d = din("vones", [128, NWT, 8])
    halo_d = din("halo_m", [128, 2])
    w_out_d = din("w_out", [1024, D])
    gg_d = din("gg", [1024])
    g2_d = din("g2", [D])
    w_up_d = din("w_up", [D, 2 * DFF])
    convw_d = din("convw", [128, 2 * NJ, 3])
    convb_d = din("convb", [128, 2 * NJ])
    w_down_d = din("w_down", [DFF, D])
    y_d = nc.dram_tensor("y", [NT, D], F32, kind="ExternalOutput").ap()
    xmid_d = nc.dram_tensor("xmid_scr", [E, D], F32, kind="Internal").ap()

    S = Sched()
    A = S.add
    out_ops = []

    with ExitStack() as st:
        ARENA = 212736
        arena = st.enter_context(nc.sbuf_tensor("arena", [128, ARENA // 2], BF16))
        psum = st.enter_context(nc.psum_tensor("psum", [128, 4096], F32))
        psum_bf = psum[:].bitcast(BF16)

        class Bump:
            def __init__(self):
                self.off = 0

            def alloc(self, shape, dt):
                esz = 4 if dt == F32 else 2
                n = int(np.prod(shape[1:]))
                nb = (n * esz + 31) // 32 * 32
                assert self.off + nb <= ARENA, ("SBUF overflow", self.off + nb)
                ap = arena[:, self.off // 2: self.off // 2 + n * esz // 2]
                self.off += nb
                if dt == F32:
                    ap = ap.bitcast(F32)
                if len(shape) == 3:
                    ap = ap.rearrange("p (a b) -> p a b", a=shape[1])
                elif len(shape) == 4:
                    ap = ap.rearrange("p (a b c) -> p a b c", a=shape[1], b=shape[2])
                return ap

        M = Bump()

        def bank(b, n=512, off=0):
            return psum[:, b * 512 + off: b * 512 + off + n]

        def bank_bf(b, n):
            return psum_bf[:, b * 1024: b * 1024 + n]

        uid = [0]

        def key(prefix):
            uid[0] += 1
            return (prefix, uid[0])

        ident = M.alloc([128, 128], BF16)
        identf = M.alloc([128, 128], F32)
        bones = M.alloc([128, 128], BF16)
        bonesf = M.alloc([128, 128], F32)
        gq = M.alloc([128, 6], F32)
        g1rep = M.alloc([128, D], F32)
        halo_m = M.alloc([128, 2], F32)
        A("pool", lambda e: e.memset(identf, 0.0), w=["identf"])
        A("pool", lambda e: e.affine_select(out=identf, in_=identf, pattern=[[-1, 128]], compare_op=ALU.not_equal,
                                            fill=1.0, base=0, channel_multiplier=1), r=["identf"], w=["identf"])
        A("pool", lambda e: e.tensor_copy(out=ident, in_=identf), r=["identf"], w=["ident"])
        A("pool", lambda e: e.memset(bonesf, 0.0), w=["bonesf"])
        A("pool", lambda e: e.memset(bonesf[0:64, 0:64], 1.0), r=["bonesf"], w=["bonesf"])
        A("pool", lambda e: e.memset(bonesf[64:128, 64:128], 1.0), r=["bonesf"], w=["bonesf"])
        A("pool", lambda e: e.tensor_copy(out=bones, in_=bonesf), r=["bonesf"], w=["bones"])
        epsc = M.alloc([128, 2], F32)
        onesf = M.alloc([128, CH], F32)
        A("pool", lambda e: e.memset(epsc[:, 0:1], EPS), w=["epsc"])
        A("pool", lambda e: e.memset(epsc[:, 1:2], 64 * EPS), w=["epsc"])
        A("pool", lambda e: e.memset(onesf, 1.0), w=["onesf"])
        A("sp", lambda e: e.dma_start(out=gq, in_=gq_d), w=["gq"], dma=True)
        A("sp", lambda e: e.dma_start(out=g1rep, in_=g1_d.partition_broadcast(128)), w=["g1rep"], dma=True)
        A("sp", lambda e: e.dma_start(out=halo_m, in_=halo_d), w=["halo_m"], dma=True)

        stage = [M.alloc([128, 512], F32) for _ in range(2)]
        qat_off = M.off
        QAT = M.alloc([128, 4, E], BF16)
        mark_persist = M.off

        cast_rr = [0]

        def load_cast(dst, src, rows, ncols, dkey, stage):
            for kc in range(rows // 128):
                for c0 in range(0, ncols, 512):
                    cn = min(512, ncols - c0)
                    i = cast_rr[0] % 2
                    cast_rr[0] += 1
                    sk = ("stage", i)
                    A("sp", lambda e, kc=kc, c0=c0, cn=cn, i=i: e.dma_start(
                        out=stage[i][:, 0:cn], in_=src[kc * 128:(kc + 1) * 128, c0:c0 + cn]), w=[sk], dma=True)
                    eng = ("pool", "act")[cast_rr[0] % 2]
                    if eng == "pool":
                        A("pool", lambda e, kc=kc, c0=c0, cn=cn, i=i: e.tensor_copy(out=dst[:, kc, c0:c0 + cn], in_=stage[i][:, 0:cn]),
                          r=[sk], w=[dkey])
                    else:
                        A("act", lambda e, kc=kc, c0=c0, cn=cn, i=i: e.activation(out=dst[:, kc, c0:c0 + cn], in_=stage[i][:, 0:cn], func=AF.Copy),
                          r=[sk], w=[dkey])

        def rstd_ops(out_ap, in_ap, scale, epscol, rkeys, wkey, nparts=128):
            if USE_LN:
                A("act", lambda e: e.activation(out=out_ap, in_=in_ap, func=AF.Ln, scale=scale, bias=epsc[0:nparts, epscol:epscol + 1]),
                  r=list(rkeys) + ["epsc"], w=[wkey])
                A("act", lambda e: e.activation(out=out_ap, in_=out_ap, func=AF.Exp, scale=-0.5), r=[wkey], w=[wkey])
            else:
                A("act", lambda e: e.activation(out=out_ap, in_=in_ap, func=AF.Sqrt, scale=scale, bias=(EPS, 64 * EPS)[epscol]),
                  r=list(rkeys), w=[wkey])
                A("dve", lambda e: e.reciprocal(out=out_ap, in_=out_ap), r=[wkey], w=[wkey])

        def sweep(xsrc, ntok, wk, per_chunk):
            xb = [wk.alloc([128, D], F32) for _ in range(2)]
            junk = wk.alloc([128, D], BF16)
            hnb = [wk.alloc([128, D], BF16) for _ in range(2)]
            ssq = [wk.alloc([128, 1], F32) for _ in range(2)]
            rs = [wk.alloc([128, 1], F32) for _ in range(2)]
            hnT = [wk.alloc([128, KC, CH], BF16) for _ in range(2)]
            tno = 0
            pending = None
            for ci, k0 in enumerate(range(0, ntok, CH)):
                n = min(CH, ntok - k0)
                hk = ("hnT", ci % 2)
                for tt in range(n // 128):
                    i = tno % 2
                    r0 = k0 + tt * 128
                    A("sp", lambda e, i=i, r0=r0: e.dma_start(out=xb[i], in_=xsrc[r0:r0 + 128, :]), w=[("xb", i)], dma=True)
                    A("act", lambda e, i=i: e.activation(out=junk, in_=xb[i], func=AF.Square, accum_out=ssq[i]),
                      r=[("xb", i)], w=[("ssq", i), "junk"])
                    rstd_ops(rs[i], ssq[i], 1.0 / D, 0, [("ssq", i)], ("rs", i))
                    A("dve", lambda e, i=i: e.scalar_tensor_tensor(out=hnb[i], in0=xb[i], scalar=rs[i][:, 0:1], in1=g1rep,
                                                                   op0=ALU.mult, op1=ALU.mult),
                      r=[("xb", i), ("rs", i), "g1rep"], w=[("hnb", i)])
                    pb = tno % 2
                    for kc in range(KC):
                        A("pe", lambda e, i=i, kc=kc, pb=pb: e.transpose(out=bank_bf(pb, KC * 128)[:, kc * 128:(kc + 1) * 128],
                                                                        in_=hnb[i][:, kc * 128:(kc + 1) * 128], identity=ident),
                          r=[("hnb", i), "ident"], w=[("ps", pb)])
                    A("act", lambda e, ci=ci, tt=tt, pb=pb: e.activation(
                        out=hnT[ci % 2][:, :, tt * 128:(tt + 1) * 128],
                        in_=bank_bf(pb, KC * 128).rearrange("p (a b) -> p a b", a=KC), func=AF.Copy), r=[("ps", pb)], w=[hk])
                    tno += 1
                if pending is not None:
                    per_chunk(*pending)
                pending = (ci, k0, n, hnT[ci % 2], hk)
            if pending is not None:
                per_chunk(*pending)

        def proj_fm(wt, wcol, hT_, hk, c_lo, nq, pbank, wkey):
            for kc in range(KC):
                A("pe", lambda e, kc=kc: e.matmul(bank(pbank, nq), lhsT=wt[:, kc, wcol:wcol + 128], rhs=hT_[:, kc, c_lo:c_lo + nq],
                                                  start=(kc == 0), stop=(kc == KC - 1)), r=[hk, wkey], w=[("ps", pbank)])

        class NormWork:
            def __init__(self, wk):
                self.sqb = [wk.alloc([128, CH], BF16) for _ in range(2)]
                self.rs = [wk.alloc([128, CH], F32) for _ in range(2)]
                self.t1 = [wk.alloc([128, CH], F32) for _ in range(2)]
                self.t2 = [wk.alloc([128, CH], F32) for _ in range(2)]
                self.n = 0

        def qknorm(nw, pq, pqs, nq, gcol, gscol, cos, sin, roper, out_ap, okey):
            i = nw.n % 2
            nw.n += 1
            sqb, rs, t1, t2 = nw.sqb[i], nw.rs[i], nw.t1[i], nw.t2[i]
            A("act", lambda e: e.activation(out=sqb[:, 0:nq], in_=bank(pq, nq), func=AF.Square), r=[("ps", pq)], w=[("sqb", i)])
            if pqs is None:
                A("dve", lambda e: e.scalar_tensor_tensor(out=t1[:, 0:nq], in0=bank(pq, nq), scalar=gq[:, gcol:gcol + 1], in1=onesf[:, 0:nq],
                                                          op0=ALU.mult, op1=ALU.mult), r=[("ps", pq), "gq", "onesf"], w=[("t1", i)])
            else:
                A("dve", lambda e: e.scalar_tensor_tensor(out=t1[:, 0:nq], in0=bank(pq, nq), scalar=gq[:, gcol:gcol + 1], in1=cos,
                                                          op0=ALU.mult, op1=ALU.mult), r=[("ps", pq), "gq"] + roper, w=[("t1", i)])
                A("dve", lambda e: e.scalar_tensor_tensor(out=t2[:, 0:nq], in0=bank(pqs, nq), scalar=gq[:, gscol:gscol + 1], in1=sin,
                                                          op0=ALU.mult, op1=ALU.mult), r=[("ps", pqs), "gq"] + roper, w=[("t2", i)])
                A("pool", lambda e: e.tensor_tensor(out=t1[:, 0:nq], in0=t1[:, 0:nq], in1=t2[:, 0:nq], op=ALU.add),
                  r=[("t1", i), ("t2", i)], w=[("t1", i)])
            A("pe", lambda e: e.matmul(bank(6, nq), lhsT=bones, rhs=sqb[:, 0:nq], start=True, stop=True),
              r=[("sqb", i), "bones"], w=[("ps", 6)])
            rstd_ops(rs[:, 0:nq], bank(6, nq), 1.0, 1, [("ps", 6)], ("nrs", i))
            A("dve", lambda e: e.tensor_tensor(out=out_ap, in0=t1[:, 0:nq], in1=rs[:, 0:nq], op=ALU.mult),
              r=[("t1", i), ("nrs", i)], w=[okey])

        def qknorm_old(nw, pq, pqs, nq, gcol, gscol, cos, sin, roper, out_ap, okey):
            i = nw.n % 2
            nw.n += 1
            sqb, rs, t1, t2 = nw.sqb[i], nw.rs[i], nw.t1[i], nw.t2[i]
            A("act", lambda e: e.activation(out=sqb[:, 0:nq], in_=bank(pq, nq), func=AF.Square), r=[("ps", pq)], w=[("sqb", i)])
            A("pe", lambda e: e.matmul(bank(6, nq), lhsT=bones, rhs=sqb[:, 0:nq], start=True, stop=True),
              r=[("sqb", i), "bones"], w=[("ps", 6)])
            A("act", lambda e: e.activation(out=rs[:, 0:nq], in_=bank(6, nq), func=AF.Sqrt, scale=1.0, bias=64 * EPS),
              r=[("ps", 6)], w=[("nrs", i)])
            A("dve", lambda e: e.reciprocal(out=rs[:, 0:nq], in_=rs[:, 0:nq]), r=[("nrs", i)], w=[("nrs", i)])
            if pqs is None:
                A("dve", lambda e: e.scalar_tensor_tensor(out=out_ap, in0=bank(pq, nq), scalar=gq[:, gcol:gcol + 1], in1=rs[:, 0:nq],
                                                          op0=ALU.mult, op1=ALU.mult), r=[("ps", pq), ("nrs", i), "gq"], w=[okey])
                return
            A("dve", lambda e: e.scalar_tensor_tensor(out=t1[:, 0:nq], in0=bank(pq, nq), scalar=gq[:, gcol:gcol + 1], in1=rs[:, 0:nq],
                                                      op0=ALU.mult, op1=ALU.mult), r=[("ps", pq), ("nrs", i), "gq"], w=[("t1", i)])
            A("dve", lambda e: e.scalar_tensor_tensor(out=t2[:, 0:nq], in0=bank(pqs, nq), scalar=gq[:, gscol:gscol + 1], in1=rs[:, 0:nq],
                                                      op0=ALU.mult, op1=ALU.mult), r=[("ps", pqs), ("nrs", i), "gq"], w=[("t2", i)])
            A("pool", lambda e: e.tensor_tensor(out=t1[:, 0:nq], in0=t1[:, 0:nq], in1=cos, op=ALU.mult), r=[("t1", i)] + roper, w=[("t1", i)])
            A("pool", lambda e: e.tensor_tensor(out=t2[:, 0:nq], in0=t2[:, 0:nq], in1=sin, op=ALU.mult), r=[("t2", i)] + roper, w=[("t2", i)])
            A("pool", lambda e: e.tensor_tensor(out=out_ap, in0=t1[:, 0:nq], in1=t2[:, 0:nq], op=ALU.add),
              r=[("t1", i), ("t2", i)], w=[okey])

        if _os.environ.get('MK_QK', 'old') == 'old':
            qknorm = qknorm_old

        class PostWork:
            def __init__(self, wk, ggrep, goff):
                self.rd = [wk.alloc([128, 8], F32) for _ in range(2)]
                self.outf = [wk.alloc([128, 512], F32) for _ in range(2)]
                self.junk = wk.alloc([128, 512], BF16)
                self.ssq = [wk.alloc([128, 1], F32) for _ in range(2)]
                self.rs = [wk.alloc([128, 1], F32) for _ in range(2)]
                self.mixb = [wk.alloc([128, 512], BF16) for _ in range(2)]
                self.ggrep = ggrep
                self.goff = goff
                self.n = 0

        def mixer_post(pw, obank0, okeys, sz, e0, mixT, mkey, tbank, tkey):
            i = pw.n % 2
            pw.n += 1
            rd, outf, mixb = pw.rd[i], pw.outf[i], pw.mixb[i]
            for g in range(2):
                ov = bank(obank0 + g, 260).rearrange("p (a b) -> p a b", a=4)
                A("dve", lambda e, g=g, ov=ov: e.reciprocal(out=rd[0:sz, g * 4:(g + 1) * 4], in_=ov[0:sz, :, 64]),
                  r=[okeys[g]], w=[("rd", i, g)])
                for j in range(4):
                    h = g * 4 + j
                    A("act", lambda e, h=h, j=j, ov=ov: e.activation(out=outf[0:sz, h * 64:(h + 1) * 64], in_=ov[0:sz, j, 0:64],
                                                                    func=AF.Copy, scale=rd[0:sz, h:h + 1]),
                      r=[okeys[g], ("rd", i, g)], w=[("outf", i)])
            A("act", lambda e: e.activation(out=pw.junk[0:sz, :], in_=outf[0:sz, :], func=AF.Square, accum_out=pw.ssq[i][0:sz, :]),
              r=[("outf", i)], w=[("pssq", i), "pjunk"])
            rstd_ops(pw.rs[i][0:sz, :], pw.ssq[i][0:sz, :], 1.0 / 512, 0, [("pssq", i)], ("prs", i), sz)
            A("dve", lambda e: e.scalar_tensor_tensor(out=mixb[0:sz, :], in0=outf[0:sz, :], scalar=pw.rs[i][0:sz, 0:1],
                                                      in1=pw.ggrep[0:sz, pw.goff:pw.goff + 512], op0=ALU.mult, op1=ALU.mult),
              r=[("outf", i), ("prs", i), "ggrep"], w=[("mixb", i)])
            tv = bank_bf(tbank, 512).rearrange("p (a b) -> p a b", a=4)
            for c in range(4):
                A("pe", lambda e, c=c: e.transpose(out=tv[:, c, 0:sz], in_=mixb[0:sz, c * 128:(c + 1) * 128], identity=ident[0:sz, 0:sz]),
                  r=[("mixb", i), "ident"], w=[tkey])
            A("dve", lambda e: e.tensor_copy(out=mixT[:, :, e0:e0 + sz], in_=tv[:, :, 0:sz]), r=[tkey], w=[mkey])

        qtiles = [(0, 1)] + [(1 + 128 * i, 128) for i in range(NT // 128)] + [(NT + 1, 1)]

        KBT = M.alloc([128, 4, WIN], BF16)
        VB = M.alloc([128, NWT, 8, 65], BF16)
        QBT = M.alloc([128, 4, E], BF16)
        mark_w = M.off
        WinB = M.alloc([128, KC, 2560], BF16)
        load_cast(WinB, w_inB, D, 2560, "WinB", stage)
        vtmp = M.alloc([128, NWT, 8], F32)
        A("sp", lambda e: e.dma_start(out=vtmp, in_=vones_d), w=["vtmp"], dma=True)
        A("pool", lambda e: e.tensor_copy(out=VB[:, :, :, 64], in_=vtmp), r=["vtmp"], w=["VBones"])
        nw = NormWork(M)
        ropebW = [[M.alloc([128, CH], F32) for _ in range(2)] for _ in range(2)]
        vcnt = [0]

        def chunk_W(ci, k0, n, hT_, hk):
            for p in range(4):
                pb = 2 + (p % 2) * 2
                proj_fm(WinB, 1536 + p * 128, hT_, hk, 0, n, pb, "WinB")
                qknorm(nw, pb, None, n, 5, None, None, None, None, KBT[:, p, k0:k0 + n], ("KBT", ci))
            for tt in range(n // 128):
                t = (k0 // 128) + tt
                for kc in range(KC):
                    A("pe", lambda e, kc=kc, tt=tt: e.matmul(bank(7, 512), lhsT=hT_[:, kc, tt * 128:(tt + 1) * 128], rhs=WinB[:, kc, 2048:2560],
                                                             start=(kc == 0), stop=(kc == KC - 1)), r=[hk, "WinB"], w=[("ps", 7)])
                A("act", lambda e, t=t: e.activation(out=VB[:, t, :, 0:64], in_=bank(7, 512).rearrange("p (a b) -> p a b", a=8), func=AF.Copy),
                  r=[("ps", 7)], w=[("VB", t)])
            c_lo = max(k0, HALO - 1)
            c_hi = min(k0 + n, HALO - 1 + E)
            if c_lo < c_hi:
                nq = c_hi - c_lo
                e_lo = c_lo - (HALO - 1)
                off = c_lo - k0
                ri = ci % 2
                rk = ("ropeq", ri)
                rb = ropebW[ri]
                A("sp", lambda e: e.dma_start(out=rb[0][:, 0:nq], in_=ropeq_d[0, :, e_lo:e_lo + nq]), w=[rk], dma=True)
                A("sp", lambda e: e.dma_start(out=rb[1][:, 0:nq], in_=ropeq_d[1, :, e_lo:e_lo + nq]), w=[rk], dma=True)
                for p in range(4):
                    pb = 2 + (p % 2) * 2
                    proj_fm(WinB, 1024 + p * 128, hT_, hk, off, nq, pb, "WinB")
                    qknorm(nw, pb, None, nq, 4, None, None, None, None, QBT[:, p, e_lo:e_lo + nq], "QBT")
                for p in range(4):
                    pb = 2 + (p % 2) * 2
                    proj_fm(WinB, p * 128, hT_, hk, off, nq, pb, "WinB")
                    proj_fm(WinB, 512 + p * 128, hT_, hk, off, nq, pb + 1, "WinB")
                    qknorm(nw, pb, pb + 1, nq, 0, 1, rb[0][:, 0:nq], rb[1][:, 0:nq], [rk],
                           QAT[:, p, e_lo:e_lo + nq], "QAT")

        sweep(xwin, WIN, M, chunk_W)
        S.barrier()

        M.off = mark_w
        mixTB = M.alloc([128, 4, E], BF16)
        mixTB_end = M.off
        maskB = M.alloc([128, 8, MW], BF16)
        A("sp", lambda e: e.dma_start(out=maskB, in_=maskB_d), w=["maskB"], dma=True)
        ggrep = M.alloc([128, 1024], F32)
        A("sp", lambda e: e.dma_start(out=ggrep, in_=gg_d.partition_broadcast(128)), w=["ggrep"], dma=True)
        qz = [M.alloc([128, 8, 128], BF16) for _ in range(2)]
        Eb = [M.alloc([128, 8, 128], BF16) for _ in range(3)]
        PT = [M.alloc([128, 8, 128], BF16) for _ in range(3)]
        for i in range(2):
            A("pool", lambda e, i=i: e.memset(qz[i], 0.0), w=[("qz", i)])
        pwB = PostWork(M, ggrep, 512)
        anyKBT = [("KBT", ci) for ci in range((WIN + CH - 1) // CH)]
        def tile_B(qi, e0, sz):
            zi = qi % 2
            kq0 = e0 + HALO - 1
            kt_lo = max(0, (kq0 - 1024) // 128)
            kt_hi = min(NWT - 1, (kq0 + sz - 1 + 1024) // 128)
            qv = qz[zi].rearrange("p (a b) c -> p a b c", b=2)
            A("pool", lambda e, qv=qv, e0=e0, sz=sz: e.tensor_copy(out=qv[0:64, :, 0, 0:sz], in_=QBT[0:64, :, e0:e0 + sz]),
              r=["QBT"], w=[("qz", zi)])
            A("pool", lambda e, qv=qv, e0=e0, sz=sz: e.tensor_copy(out=qv[64:128, :, 1, 0:sz], in_=QBT[64:128, :, e0:e0 + sz]),
              r=["QBT"], w=[("qz", zi)])
            ob = 6
            okeys = [("ps", 6), ("ps", 7)]
            kts = list(range(kt_lo, kt_hi + 1))

            def qk(n_, kt):
                sb_ = (n_ % DEPTH) * 2
                sv = psum[:, sb_ * 512: sb_ * 512 + 1024].rearrange("p (a b) -> p a b", a=8)
                for h in range(8):
                    A("pe", lambda e, h=h, kt=kt, sv=sv: e.matmul(sv[:, h, 0:sz], lhsT=KBT[:, h // 2, kt * 128:(kt + 1) * 128],
                                                                  rhs=qz[zi][:, h, 0:sz], start=True, stop=True),
                      r=[("qz", zi)] + anyKBT[(kt * 128) // CH:(kt * 128) // CH + 1], w=[("S", n_ % DEPTH)])

            def ex(n_, kt):
                sb_ = (n_ % DEPTH) * 2
                sv = psum[:, sb_ * 512: sb_ * 512 + 1024].rearrange("p (a b) -> p a b", a=8)
                bi = n_ % DEPTH
                A("act", lambda e, sv=sv, bi=bi: e.activation(out=Eb[bi][:, :, 0:sz], in_=sv[:, :, 0:sz], func=AF.Exp, scale=8.0),
                  r=[("S", n_ % DEPTH)], w=[("Eb", bi)])
                c0 = kq0 - 128 * kt + C0
                A("dve", lambda e, bi=bi, c0=c0: e.tensor_tensor(out=PT[bi][:, :, 0:sz], in0=Eb[bi][:, :, 0:sz],
                                                               in1=maskB[:, :, c0:c0 + sz], op=ALU.mult),
                  r=[("Eb", bi), "maskB"], w=[("PT", bi)])

            def pv(n_, kt):
                bi = n_ % DEPTH
                for h in range(8):
                    g, j = h // 4, h % 4
                    A("pe", lambda e, h=h, g=g, j=j, kt=kt, bi=bi, n_=n_: e.matmul(
                        bank(ob + g, 65, j * 65)[0:sz, :], lhsT=PT[bi][:, h, 0:sz], rhs=VB[:, kt, h, :],
                        start=(n_ == 0 and j == 0), stop=(n_ == len(kts) - 1 and j == 3)),
                      r=[("PT", bi), ("VB", kt), "VBones"], w=[okeys[g]])

            for n_, kt in enumerate(kts):
                qk(n_, kt)
                if n_ >= DEPTH - 1:
                    pv(n_ - (DEPTH - 1), kts[n_ - (DEPTH - 1)])
                ex(n_, kt)
            for r_ in range(max(0, len(kts) - (DEPTH - 1)), len(kts)):
                pv(r_, kts[r_])
            mixer_post(pwB, ob, okeys, sz, e0, mixTB, "mixTB", 0, ("S", 0))

        for qi, (e0, sz) in enumerate(qtiles):
            tile_B(qi, e0, sz)
        S.barrier()

        M.off = mark_persist
        hT = M.alloc([128, KC, E], BF16)
        KAT = M.alloc([128, SEQ], BF16)
        VA = M.alloc([128, NKT, 2, 65], BF16)
        mixTA = M.alloc([128, 4, E], BF16)
        M.off = max(M.off, mixTB_end)
        mark_a = M.off
        WinA = M.alloc([128, KC, 384], BF16)
        load_cast(WinA, w_inA, D, 384, "WinA", stage)
        A("pool", lambda e: e.memset(VA[:, :, :, 64], 1.0), w=["VAones"])
        nw = NormWork(M)
        ropebA = [[M.alloc([128, CH], F32) for _ in range(2)] for _ in range(2)]

        def chunk_A(ci, k0, n, hT_, hk):
            ri = ci % 2
            rk = ("ropek", ri)
            rb = ropebA[ri]
            A("sp", lambda e: e.dma_start(out=rb[0][:, 0:n], in_=ropek_d[0, :, k0:k0 + n]), w=[rk], dma=True)
            A("sp", lambda e: e.dma_start(out=rb[1][:, 0:n], in_=ropek_d[1, :, k0:k0 + n]), w=[rk], dma=True)
            pb = 2 + (ci % 2) * 2
            proj_fm(WinA, 0, hT_, hk, 0, n, pb, "WinA")
            proj_fm(WinA, 128, hT_, hk, 0, n, pb + 1, "WinA")
            qknorm(nw, pb, pb + 1, n, 2, 3, rb[0][:, 0:n], rb[1][:, 0:n], [rk], KAT[:, k0:k0 + n], ("KAT", ci))
            for tt in range(n // 128):
                t = (k0 // 128) + tt
                for kc in range(KC):
                    A("pe", lambda e, kc=kc, tt=tt: e.matmul(bank(7, 128), lhsT=hT_[:, kc, tt * 128:(tt + 1) * 128], rhs=WinA[:, kc, 256:384],
                                                             start=(kc == 0), stop=(kc == KC - 1)), r=[hk, "WinA"], w=[("ps", 7)])
                A("act", lambda e, t=t: e.activation(out=VA[:, t, :, 0:64], in_=bank(7, 128).rearrange("p (a b) -> p a b", a=2), func=AF.Copy),
                  r=[("ps", 7)], w=[("VA", t)])

        sweep(xfull, SEQ, M, chunk_A)
        S.barrier()

        M.off = mark_a
        Wout = M.alloc([128, 8, D], BF16)
        load_cast(Wout, w_out_d, 1024, D, "Wout", stage)
        ggrepA = M.alloc([128, 1024], F32)
        A("sp", lambda e: e.dma_start(out=ggrepA, in_=gg_d.partition_broadcast(128)), w=["ggrep"], dma=True)
        g2rep = M.alloc([128, D], F32)
        A("sp", lambda e: e.dma_start(out=g2rep, in_=g2_d.partition_broadcast(128)), w=["g2rep"], dma=True)
        qzA = [M.alloc([128, 2, 4, 128], BF16) for _ in range(2)]
        PTA = [M.alloc([128, 2, 4, 128], BF16) for _ in range(3)]
        for i in range(2):
            A("pool", lambda e, i=i: e.memset(qzA[i], 0.0), w=[("qzA", i)])
        pwA = PostWork(M, ggrepA, 0)
        xr = [M.alloc([128, D], F32) for _ in range(2)]
        xm = [M.alloc([128, D], F32) for _ in range(2)]
        hb = [M.alloc([128, D], BF16) for _ in range(2)]
        junk2 = M.alloc([128, D], BF16)
        ssq2 = [M.alloc([128, 1], F32) for _ in range(2)]
        rs2 = [M.alloc([128, 1], F32) for _ in range(2)]
        anyKAT = [("KAT", ci) for ci in range(SEQ // CH)]
        def tile_A(qi, e0, sz):
            zi = qi % 2
            kq0 = e0 + HALO - 1
            A("pool", lambda e, e0=e0, sz=sz, zi=zi: e.tensor_copy(out=qzA[zi][0:64, 0, :, 0:sz], in_=QAT[0:64, :, e0:e0 + sz]),
              r=["QAT"], w=[("qzA", zi)])
            A("pool", lambda e, e0=e0, sz=sz, zi=zi: e.tensor_copy(out=qzA[zi][64:128, 1, :, 0:sz], in_=QAT[64:128, :, e0:e0 + sz]),
              r=["QAT"], w=[("qzA", zi)])
            A("sp", lambda e, zi=zi, kq0=kq0, sz=sz: e.dma_start(out=xr[zi][0:sz, :], in_=xwin[kq0:kq0 + sz, :]), w=[("xr", zi)], dma=True)
            ob = 6
            okeys = [("ps", 6), ("ps", 7)]

            def qkA(kt):
                sb_ = (kt % DEPTH) * 2
                sv = psum[:, sb_ * 512: sb_ * 512 + 1024].rearrange("p (g a b) -> p g a b", g=2, a=4)
                for g in range(2):
                    A("pe", lambda e, g=g, kt=kt, sv=sv: e.matmul(sv[:, g, :, 0:sz], lhsT=KAT[:, kt * 128:(kt + 1) * 128],
                                                                  rhs=qzA[zi][:, g, :, 0:sz], start=True, stop=True),
                      r=[("qzA", zi), anyKAT[(kt * 128) // CH]], w=[("S", kt % DEPTH)])

            def exA(kt):
                sb_ = (kt % DEPTH) * 2
                sv = psum[:, sb_ * 512: sb_ * 512 + 1024].rearrange("p (g a b) -> p g a b", g=2, a=4)
                bi = kt % DEPTH
                A("act", lambda e, sv=sv, bi=bi: e.activation(out=PTA[bi][:, :, :, 0:sz], in_=sv[:, :, :, 0:sz], func=AF.Exp, scale=8.0),
                  r=[("S", kt % DEPTH)], w=[("PTA", bi)])

            def pvA(kt):
                bi = kt % DEPTH
                for g in range(2):
                    for j in range(4):
                        A("pe", lambda e, g=g, j=j, kt=kt, bi=bi: e.matmul(
                            bank(ob + g, 65, j * 65)[0:sz, :], lhsT=PTA[bi][:, g, j, 0:sz], rhs=VA[:, kt, g, :],
                            start=(kt == 0 and j == 0), stop=(kt == NKT - 1 and j == 3)),
                          r=[("PTA", bi), ("VA", kt), "VAones"], w=[okeys[g]])

            for kt in range(NKT):
                qkA(kt)
                if kt >= DEPTH - 1:
                    pvA(kt - (DEPTH - 1))
                exA(kt)
            for r_ in range(NKT - (DEPTH - 1), NKT):
                pvA(r_)
            mixer_post(pwA, ob, okeys, sz, e0, mixTA, "mixTA", 2, ("S", 1))
            for nb_ in range(NB):
                for kc in range(8):
                    src = mixTA[:, kc] if kc < 4 else mixTB[:, kc - 4]
                    A("pe", lambda e, nb_=nb_, kc=kc, src=src: e.matmul(bank(nb_, 512)[0:sz, :], lhsT=src[:, e0:e0 + sz],
                                                                        rhs=Wout[:, kc, nb_ * 512:(nb_ + 1) * 512],
                                                                        start=(kc == 0), stop=(kc == 7)),
                      r=["mixTA", "mixTB", "Wout"], w=[("S", nb_ // 2)] if NB > 1 else [("S", 0)])
            ykeys = [("S", nb_ // 2) for nb_ in range(NB)] if NB > 1 else [("S", 0)]
            for nb_ in range(NB):
                A("dve", lambda e, nb_=nb_, zi=zi: e.tensor_tensor(out=xm[zi][0:sz, nb_ * 512:(nb_ + 1) * 512], in0=bank(nb_, 512)[0:sz, :],
                                                                   in1=xr[zi][0:sz, nb_ * 512:(nb_ + 1) * 512], op=ALU.add),
                  r=[ykeys[nb_], ("xr", zi)], w=[("xm", zi)])
            A("sp", lambda e, zi=zi, e0=e0, sz=sz: e.dma_start(out=xmid_d[e0:e0 + sz, :], in_=xm[zi][0:sz, :]),
              r=[("xm", zi)], w=[("xmid_d", e0)], dma=True)
            A("act", lambda e, zi=zi: e.activation(out=junk2[0:sz, :], in_=xm[zi][0:sz, :], func=AF.Square, accum_out=ssq2[zi][0:sz, :]),
              r=[("xm", zi)], w=[("ssq2", zi), "junk2"])
            rstd_ops(rs2[zi][0:sz, :], ssq2[zi][0:sz, :], 1.0 / D, 0, [("ssq2", zi)], ("rs2", zi), sz)
            A("dve", lambda e, zi=zi: e.scalar_tensor_tensor(out=hb[zi][0:sz, :], in0=xm[zi][0:sz, :], scalar=rs2[zi][0:sz, 0:1],
                                                             in1=g2rep[0:sz, :], op0=ALU.mult, op1=ALU.mult),
              r=[("xm", zi), ("rs2", zi), "g2rep"], w=[("hb", zi)])
            tv = bank_bf(4, KC * 128).rearrange("p (a b) -> p a b", a=KC)
            for kc in range(KC):
                A("pe", lambda e, kc=kc, zi=zi: e.transpose(out=tv[:, kc, 0:sz], in_=hb[zi][0:sz, kc * 128:(kc + 1) * 128],
                                                            identity=ident[0:sz, 0:sz]), r=[("hb", zi), "ident"], w=[("S", 2)])
            A("dve", lambda e, e0=e0, sz=sz: e.tensor_copy(out=hT[:, :, e0:e0 + sz], in_=tv[:, :, 0:sz]), r=[("S", 2)], w=["hT"])

        for qi, (e0, sz) in enumerate(qtiles):
            tile_A(qi, e0, sz)
        A("act", lambda e: e.activation(out=hT[:, :, 0:1], in_=hT[:, :, 0:1], func=AF.Copy, scale=halo_m[:, 0:1]),
          r=["hT", "halo_m"], w=["hT"])
        A("act", lambda e: e.activation(out=hT[:, :, E - 1:E], in_=hT[:, :, E - 1:E], func=AF.Copy, scale=halo_m[:, 1:2]),
          r=["hT", "halo_m"], w=["hT"])
        S.barrier()

        M.off = mark_persist + (KC * E * 2 + 31) // 32 * 32
        actT = M.alloc([128, NJ, HN], BF16)
        Wd = M.alloc([128, NJ, D], BF16)
        convw = M.alloc([128, 2 * NJ, 3], F32)
        convb = M.alloc([128, 2 * NJ], F32)
        A("sp", lambda e: e.dma_start(out=convw, in_=convw_d), w=["convw"], dma=True)
        A("sp", lambda e: e.dma_start(out=convb, in_=convb_d), w=["convb"], dma=True)
        load_cast(Wd, w_down_d, DFF, D, "Wd", stage)
        wst = [M.alloc([128, KC, 256], F32) for _ in range(2)]
        wub = [M.alloc([128, KC, 256], BF16) for _ in range(2)]
        cg = [M.alloc([128, HN], F32) for _ in range(2)]
        cv = [M.alloc([128, HN], F32) for _ in range(2)]
        tb = [M.alloc([128, HN], F32) for _ in range(2)]
        bconst = M.alloc([128, 1], F32)
        A("pool", lambda e: e.memset(bconst, 1.5957691216057308), w=["bconst"])
        if qat_off + 4 * D * 4 <= mark_persist:
            _save = M.off
            M.off = qat_off
            xmF = [M.alloc([128, D], F32) for _ in range(2)]
            yt = [M.alloc([128, D], F32) for _ in range(2)]
            M.off = _save
        else:
            xmF = [M.alloc([128, D], F32) for _ in range(2)]
            yt = [M.alloc([128, D], F32) for _ in range(2)]
        W = HN + 2
        pieces = [(c, min(512, W - c)) for c in range(0, W, 512)]
        wup_v = w_up_d.rearrange("(kc p) c -> p kc c", p=128)

        def ffn_j(hf, j, it):
            eb = HN * hf
            wi = it % 2
            A("sp", lambda e: e.dma_start(out=wst[wi][:, :, 0:128], in_=wup_v[:, :, j * 128:(j + 1) * 128]),
              w=[("wst", wi)], dma=True)
            A("sp", lambda e: e.dma_start(out=wst[wi][:, :, 128:256], in_=wup_v[:, :, DFF + j * 128:DFF + (j + 1) * 128]),
              w=[("wst", wi)], dma=True)
            if SPLIT_CAST:
                A("act", lambda e: e.activation(out=wub[wi][:, 0:KC // 2, :], in_=wst[wi][:, 0:KC // 2, :], func=AF.Copy),
                  r=[("wst", wi)], w=[("wub", wi)])
                A("pool", lambda e: e.tensor_copy(out=wub[wi][:, KC // 2:KC, :], in_=wst[wi][:, KC // 2:KC, :]),
                  r=[("wst", wi)], w=[("wubp", wi)])
            else:
                A("act", lambda e: e.activation(out=wub[wi], in_=wst[wi], func=AF.Copy), r=[("wst", wi)], w=[("wub", wi), ("wubp", wi)])
            for half, b0 in ((0, 0), (1, 3)):
                for pi, (c, cn) in enumerate(pieces):
                    for kc in range(KC):
                        A("pe", lambda e, kc=kc, c=c, cn=cn, b0=b0, pi=pi, half=half: e.matmul(
                            bank(b0 + pi, cn), lhsT=wub[wi][:, kc, half * 128:(half + 1) * 128], rhs=hT[:, kc, eb + c:eb + c + cn],
                            start=(kc == 0), stop=(kc == KC - 1)), r=[("wub", wi), ("wubp", wi), "hT"], w=[("G", half)])
            ci_ = j % 2
            for half, b0, cbuf in ((0, 0, cg[ci_]), (1, 3, cv[ci_])):
                jj = j + half * NJ
                P0 = psum[:, b0 * 512: b0 * 512 + W]
                ck = ("c", half, ci_)
                A("act", lambda e, P0=P0, cbuf=cbuf, jj=jj: e.activation(out=cbuf, in_=P0[:, 1:HN + 1], func=AF.Identity,
                                                                        scale=convw[:, jj, 1:2], bias=convb[:, jj:jj + 1]),
                  r=[("G", half), "convw", "convb"], w=[ck])
                A("dve", lambda e, P0=P0, cbuf=cbuf, jj=jj: e.scalar_tensor_tensor(out=cbuf, in0=P0[:, 0:HN], scalar=convw[:, jj, 0:1],
                                                                                  in1=cbuf, op0=ALU.mult, op1=ALU.add),
                  r=[("G", half), "convw", ck], w=[ck])
                A("dve", lambda e, P0=P0, cbuf=cbuf, jj=jj: e.scalar_tensor_tensor(out=cbuf, in0=P0[:, 2:HN + 2], scalar=convw[:, jj, 2:3],
                                                                                  in1=cbuf, op0=ALU.mult, op1=ALU.add),
                  r=[("G", half), "convw", ck], w=[ck])
            cgb, cvb = cg[ci_], cv[ci_]
            gk, vk = ("c", 0, ci_), ("c", 1, ci_)
            tbi = tb[ci_]
            tk = ("tb", ci_)
            A("act", lambda e: e.activation(out=tbi, in_=cgb, func=AF.Square, scale=math.sqrt(0.0713548162726)), r=[gk], w=[tk])
            A("dve", lambda e: e.scalar_tensor_tensor(out=tbi, in0=tbi, scalar=bconst[:, 0:1], in1=cgb, op0=ALU.add, op1=ALU.mult),
              r=[tk, gk, "bconst"], w=[tk])
            A("act", lambda e: e.activation(out=tbi, in_=tbi, func=AF.Sigmoid), r=[tk], w=[tk])
            A("pool", lambda e: e.tensor_tensor(out=cvb, in0=cgb, in1=cvb, op=ALU.mult), r=[gk, vk], w=[vk])
            A("dve", lambda e: e.tensor_tensor(out=actT[:, j, :], in0=cvb, in1=tbi, op=ALU.mult), r=[vk, tk], w=["actT"])

        def down_tile(hf, ti):
            eb = HN * hf
            zi = ti % 2
            e0 = 1 + eb + 128 * ti
            A("sp", lambda e: e.dma_start(out=xmF[zi], in_=xmid_d[e0:e0 + 128, :]),
              r=[("xmid_d", e0)], w=[("xmF", zi)], dma=True)
            for nb_ in range(NB):
                for j in range(NJ):
                    A("pe", lambda e, nb_=nb_, j=j: e.matmul(bank(6 + nb_ % 2, 512), lhsT=actT[:, j, ti * 128:(ti + 1) * 128],
                                                             rhs=Wd[:, j, nb_ * 512:(nb_ + 1) * 512],
                                                             start=(j == 0), stop=(j == NJ - 1)),
                      r=["actT", "Wd"], w=[("ps", 6 + nb_ % 2)])
                A("dve", lambda e, nb_=nb_: e.tensor_tensor(out=yt[zi][:, nb_ * 512:(nb_ + 1) * 512], in0=bank(6 + nb_ % 2, 512),
                                                            in1=xmF[zi][:, nb_ * 512:(nb_ + 1) * 512], op=ALU.add),
                  r=[("ps", 6 + nb_ % 2), ("xmF", zi)], w=[("yt", zi)])
            out_ops.append(A("sp", lambda e: e.dma_start(out=y_d[e0 - 1:e0 - 1 + 128, :], in_=yt[zi]),
                             r=[("yt", zi)], dma=True))

        it = 0
        for hf in range(2):
            for j in range(NJ):
                ffn_j(hf, j, it)
                it += 1
            for ti in range(HN // 128):
                down_tile(hf, ti)
        S.emit(nc, st, final_wait_ops=out_ops)
    return nc


def _swap_idx():
    idx = np.arange(64)
    sw = idx.copy()
    sw[0:16], sw[16:32], sw[32:48], sw[48:64] = idx[16:32], idx[0:16], idx[48:64], idx[32:48]
    return sw


def _rope_tables(pos):
    pos = np.asarray(pos, dtype=np.float64)
    valid = pos >= 0
    p = np.where(valid, pos, 0)
    row = np.floor(p / 64).astype(np.float32)
    col = (p % 64).astype(np.float32)
    inv = (np.float32(10000.0) ** (-np.arange(0, 32, 2, dtype=np.float32) / np.float32(32))).astype(np.float32)
    ar = (row[:, None] * inv[None, :]).astype(np.float32)
    ac = (col[:, None] * inv[None, :]).astype(np.float32)
    cos = np.concatenate([np.cos(ar), np.cos(ar), np.cos(ac), np.cos(ac)], axis=1).T
    sin = np.concatenate([-np.sin(ar), np.sin(ar), -np.sin(ac), np.sin(ac)], axis=1).T
    cos = np.concatenate([cos, cos], axis=0)
    sin = np.concatenate([sin, sin], axis=0)
    return np.ascontiguousarray(np.stack([cos, sin]).astype(np.float32))


def _maskB():
    ik = np.arange(128)[:, None]
    c = np.arange(MW)[None, :]
    dl = (c - C0 - ik).astype(np.int64)
    ad = np.abs(dl)
    m = (ad <= 64).astype(np.float64) + ((ad <= 256) & (dl % 4 == 0)) + ((ad <= 1024) & (dl % 16 == 0))
    out = np.zeros((128, 8, MW), np.float32)
    for h in range(8):
        slope = 2.0 ** (-8.0 * (h + 1) / 8)
        out[:, h, :] = m * np.exp(-slope * ad)
    return out.astype(ml_dtypes.bfloat16)


_CACHE = {}


def _run(x, norm1_g, w_in, qa_norm_g, ka_norm_g, qb_norm_g, kb_norm_g, outa_norm_g, outb_norm_g,
         w_out, norm2_g, w_up, conv_w, conv_b, w_down, runner=None):
    f = lambda a: np.ascontiguousarray(np.asarray(a, dtype=np.float32))
    x, w_in, w_out, w_up, w_down = f(x), f(w_in), f(w_out), f(w_up), f(w_down)
    B, SEQ, D = x.shape
    DFF = w_down.shape[0]
    NT = SEQ // 4
    NJ = DFF // 128
    WIN = NT + 2 * HALO
    NWT = WIN // 128
    E = NT + 2
    sw = _swap_idx()
    qa = w_in[:, 0:512].reshape(D, 8, 64)
    pair = np.stack([np.concatenate([qa[:, p], qa[:, 4 + p]], axis=1) for p in range(4)], axis=1)
    pair_s = np.stack([np.concatenate([qa[:, p][:, sw], qa[:, 4 + p][:, sw]], axis=1) for p in range(4)], axis=1)
    ka = w_in[:, 512:640].reshape(D, 2, 64)
    ka_s = ka[:, :, sw]
    va = w_in[:, 640:768]
    qb, kb, vb = w_in[:, 768:1280], w_in[:, 1280:1792], w_in[:, 1792:2304]
    w_inB = np.ascontiguousarray(np.concatenate([pair.reshape(D, 512), pair_s.reshape(D, 512), qb, kb, vb], axis=1))
    w_inA = np.ascontiguousarray(np.concatenate([ka.reshape(D, 128), ka_s.reshape(D, 128), va], axis=1))
    d2 = lambda g: np.concatenate([f(g), f(g)])
    gq = np.ascontiguousarray(np.stack([d2(qa_norm_g), d2(f(qa_norm_g)[sw]), d2(ka_norm_g), d2(f(ka_norm_g)[sw]),
                                        d2(qb_norm_g), d2(kb_norm_g)], axis=1))
    gg = np.ascontiguousarray(np.concatenate([f(outa_norm_g), f(outb_norm_g)]))
    convw = np.ascontiguousarray(f(conv_w).T.reshape(2 * NJ, 128, 3).transpose(1, 0, 2))
    convb = np.ascontiguousarray(f(conv_b).reshape(2 * NJ, 128).T)
    ropek = _rope_tables(np.arange(SEQ))
    maskB = _maskB()
    shared = dict(w_inB=w_inB, w_inA=w_inA, gq=gq, g1=f(norm1_g), ropek=ropek, maskB=maskB, w_out=w_out, gg=gg,
                  g2=f(norm2_g), w_up=w_up, convw=convw, convb=convb, w_down=w_down)
    in_maps = []
    for c in range(8):
        b, qtr = c // 4, c % 4
        T0 = qtr * NT
        xpad = np.zeros((SEQ + 2 * HALO, D), np.float32)
        xpad[HALO:HALO + SEQ] = x[b]
        xwin = np.ascontiguousarray(xpad[T0:T0 + WIN])
        tpos = T0 - HALO + np.arange(WIN)
        valid = ((tpos >= 0) & (tpos < SEQ)).astype(np.float32)
        vones = np.ascontiguousarray(np.repeat(valid.reshape(NWT, 128).T[:, :, None], 8, axis=2))
        epos = T0 - 1 + np.arange(E)
        ropeq = _rope_tables(np.where((epos >= 0) & (epos < SEQ), epos, -1))
        halo_m = np.zeros((128, 2), np.float32)
        halo_m[:, 0] = 1.0 if T0 - 1 >= 0 else 0.0
        halo_m[:, 1] = 1.0 if T0 + NT < SEQ else 0.0
        m = dict(shared)
        m.update(xfull=np.ascontiguousarray(x[b]), xwin=xwin, vones=vones, ropeq=ropeq, halo_m=halo_m)
        in_maps.append(m)
    ck = (D, SEQ, NT, DFF)
    if ck not in _CACHE:
        _CACHE[ck] = build(D, SEQ, NT, DFF)
    nc = _CACHE[ck]
    if runner is not None:
        outs = runner(nc, in_maps)
    else:
        res = run_bass_kernel_spmd(nc, in_maps, core_ids=list(range(8)))
        outs = [r["y"] for r in res.results]
    y = np.zeros((B, SEQ, D), np.float32)
    for c in range(8):
        b, qtr = c // 4, c % 4
        y[b, qtr * NT:(qtr + 1) * NT] = np.asarray(outs[c]).reshape(NT, D)
    return y


def kernel(**inputs):
    return _run(**inputs)
```
